# Optimizing a Trainium2 kernel written in Bass

```python
import jax, jax.numpy as jnp
from jax import lax
import numpy as np

D_MODEL = 1024
BATCH = 8
SEQ = 2048
DEPTH = 4
DEC_BATCH = 32
DEC_SEQ = 64
PAST_LEN = 4096

CHUNK = 64
N_MIXERS = 2
N_A_LAYERS = (DEPTH + 1) // 2
N_B_LAYERS = DEPTH // 2
EPS = 1e-6
NEG_INF = -1e30

A_HEADS = 8
A_HEAD_DIM = D_MODEL // A_HEADS
A_WIDTH = A_HEADS * A_HEAD_DIM
A_IN_COLS = 4 * A_WIDTH + A_HEADS
Q_BLOCK = 128

B_WIDTH = 2 * D_MODEL
B_GROUPS = 8
B_GROUP_DIM = B_WIDTH // B_GROUPS
B_IN_COLS = 3 * B_WIDTH
MLP_CHUNK = 128

kernel_name = "fox_gmlp_streaming_encoder_step"


def rmsnorm(x, g):
    xf = x.astype(jnp.float32)
    y = xf * lax.rsqrt(jnp.mean(xf * xf, axis=-1, keepdims=True) + EPS)
    return (y * g.astype(jnp.float32)).astype(x.dtype)


def fox_project(h, w_in, b_f, q_g, k_g):
    z = h @ w_in
    bsz, t = h.shape[0], h.shape[1]
    q = z[..., :A_WIDTH].reshape(bsz, t, A_HEADS, A_HEAD_DIM)
    k = z[..., A_WIDTH:2 * A_WIDTH].reshape(bsz, t, A_HEADS, A_HEAD_DIM)
    v = z[..., 2 * A_WIDTH:3 * A_WIDTH].reshape(bsz, t, A_HEADS, A_HEAD_DIM)
    gate = z[..., 3 * A_WIDTH:4 * A_WIDTH]
    f_logit = z[..., 4 * A_WIDTH:]
    q = rmsnorm(q, q_g)
    k = rmsnorm(k, k_g)
    logf = jax.nn.log_sigmoid(f_logit.astype(jnp.float32) + b_f.astype(jnp.float32))
    return q, k, v, logf, gate


def fox_attend(q, k, v, f_q, f_k, mask):
    scale = A_HEAD_DIM ** -0.5
    s = jnp.einsum('bqhd,bkhd->bhqk', q, k).astype(jnp.float32) * scale
    bias = jnp.transpose(f_q, (0, 2, 1))[..., :, None] - jnp.transpose(f_k, (0, 2, 1))[..., None, :]
    s = jnp.where(mask[None, None], s + bias, NEG_INF)
    p = jax.nn.softmax(s, axis=-1).astype(v.dtype)
    return jnp.einsum('bhqk,bkhd->bqhd', p, v)


def fox_prompt(h, w_in, b_f, q_g, k_g, w_out):
    bsz, s_len = h.shape[0], h.shape[1]
    q, k, v, logf, gate = fox_project(h, w_in, b_f, q_g, k_g)
    f_cum = jnp.cumsum(logf, axis=1)
    nb = s_len // Q_BLOCK
    q_blocks = jnp.transpose(q.reshape(bsz, nb, Q_BLOCK, A_HEADS, A_HEAD_DIM), (1, 0, 2, 3, 4))
    f_blocks = jnp.transpose(f_cum.reshape(bsz, nb, Q_BLOCK, A_HEADS), (1, 0, 2, 3))
    p_blocks = jnp.arange(s_len).reshape(nb, Q_BLOCK)
    k_pos = jnp.arange(s_len)

    def one_block(args):
        q_b, f_b, p_b = args
        mask = k_pos[None, :] <= p_b[:, None]
        return fox_attend(q_b, k, v, f_b, f_cum, mask)

    o = lax.map(one_block, (q_blocks, f_blocks, p_blocks))
    o = jnp.transpose(o, (1, 0, 2, 3, 4)).reshape(bsz, s_len, A_WIDTH)
    y = (o * jax.nn.silu(gate)) @ w_out
    return y, k, v, logf


def fox_sample(h, c_k, c_v, c_logf, w_in, b_f, q_g, k_g, w_out):
    bsz, t = h.shape[0], h.shape[1]
    p_len = c_k.shape[1]
    q, k, v, logf, gate = fox_project(h, w_in, b_f, q_g, k_g)
    f_past = jnp.cumsum(c_logf.astype(jnp.float32), axis=1)
    f_new = f_past[:, -1:, :] + jnp.cumsum(logf, axis=1)
    k_all = jnp.concatenate([c_k.astype(k.dtype), k], axis=1)
    v_all = jnp.concatenate([c_v.astype(v.dtype), v], axis=1)
    f_all = jnp.concatenate([f_past, f_new], axis=1)
    k_pos = jnp.arange(p_len + t)
    q_pos = p_len + jnp.arange(t)
    mask = k_pos[None, :] <= q_pos[:, None]
    o = fox_attend(q, k_all, v_all, f_new, f_all, mask).reshape(bsz, t, A_WIDTH)
    y = (o * jax.nn.silu(gate)) @ w_out
    return y, k, v, logf


def sgu_mask():
    c = jnp.arange(MLP_CHUNK) // CHUNK
    return c[None, :] <= c[:, None]


def gmlp_project(h, w_in, v_g):
    z = h @ w_in
    bsz, t = h.shape[0], h.shape[1]
    u = z[..., :B_WIDTH]
    v = z[..., B_WIDTH:2 * B_WIDTH].reshape(bsz, t, B_GROUPS, B_GROUP_DIM)
    gate = z[..., 2 * B_WIDTH:]
    v = rmsnorm(v, v_g)
    return u, v, gate


def gmlp_prompt(h, w_in, v_g, ws, bs, w_out):
    bsz, s_len = h.shape[0], h.shape[1]
    u, v, gate = gmlp_project(h, w_in, v_g)
    n_c = s_len // MLP_CHUNK
    vc = v.reshape(bsz, n_c, MLP_CHUNK, B_GROUPS, B_GROUP_DIM)
    w = (ws * sgu_mask()[None]).astype(v.dtype)
    mixed = jnp.einsum('gts,bcsgd->bctgd', w, vc) + jnp.transpose(bs)[None, None, :, :, None].astype(v.dtype)
    mixed = mixed.reshape(bsz, s_len, B_WIDTH)
    y = (u * mixed * jax.nn.silu(gate)) @ w_out
    return y


def gmlp_sample(h, w_in, v_g, ws, bs, w_out):
    bsz, t = h.shape[0], h.shape[1]
    u, v, gate = gmlp_project(h, w_in, v_g)
    w = (ws * sgu_mask()[None])[:, :t, :t].astype(v.dtype)
    mixed = jnp.einsum('gts,bsgd->btgd', w, v) + jnp.transpose(bs[:, :t])[None, :, :, None].astype(v.dtype)
    y = (u * mixed.reshape(bsz, t, B_WIDTH) * jax.nn.silu(gate)) @ w_out
    return y, v.reshape(bsz, t, B_WIDTH)


def setup_inputs(seed: int = 0) -> dict:
    key = jax.random.key(seed)
    ks = jax.random.split(key, 20)
    f32 = jnp.float32
    x_prompt = jax.random.normal(ks[0], (BATCH, SEQ, D_MODEL), f32)
    x_sample = jax.random.normal(ks[1], (DEC_BATCH, DEC_SEQ, D_MODEL), f32)
    cache_k = jax.random.normal(ks[2], (N_A_LAYERS, DEC_BATCH, PAST_LEN, A_HEADS, A_HEAD_DIM), f32)
    cache_v = jax.random.normal(ks[3], (N_A_LAYERS, DEC_BATCH, PAST_LEN, A_HEADS, A_HEAD_DIM), f32)
    cache_logf = jax.nn.log_sigmoid(2.5 + jax.random.normal(ks[4], (N_A_LAYERS, DEC_BATCH, PAST_LEN, A_HEADS), f32))
    norm_g = 1.0 + 0.02 * jax.random.normal(ks[5], (DEPTH, D_MODEL), f32)
    w_in_a = jax.random.normal(ks[6], (N_A_LAYERS, D_MODEL, A_IN_COLS), f32) * D_MODEL ** -0.5
    b_f = jax.random.uniform(ks[7], (N_A_LAYERS, A_HEADS), f32, minval=1.0, maxval=4.0)
    q_g = 1.0 + 0.02 * jax.random.normal(ks[8], (N_A_LAYERS, A_HEAD_DIM), f32)
    k_g = 1.0 + 0.02 * jax.random.normal(ks[9], (N_A_LAYERS, A_HEAD_DIM), f32)
    w_out_a = jax.random.normal(ks[10], (N_A_LAYERS, A_WIDTH, D_MODEL), f32) * A_WIDTH ** -0.5
    w_in_b = jax.random.normal(ks[11], (N_B_LAYERS, D_MODEL, B_IN_COLS), f32) * D_MODEL ** -0.5
    v_g = 1.0 + 0.02 * jax.random.normal(ks[12], (N_B_LAYERS, B_GROUPS, B_GROUP_DIM), f32)
    ws = jax.random.normal(ks[13], (N_B_LAYERS, B_GROUPS, MLP_CHUNK, MLP_CHUNK), f32) * MLP_CHUNK ** -0.5
    bs = 1.0 + 0.1 * jax.random.normal(ks[14], (N_B_LAYERS, B_GROUPS, MLP_CHUNK), f32)
    w_out_b = jax.random.normal(ks[15], (N_B_LAYERS, B_WIDTH, D_MODEL), f32) * B_WIDTH ** -0.5
    return {"x_prompt": x_prompt, "x_sample": x_sample,
            "cache_k": cache_k, "cache_v": cache_v, "cache_logf": cache_logf,
            "norm_g": norm_g, "w_in_a": w_in_a, "b_f": b_f, "q_g": q_g, "k_g": k_g, "w_out_a": w_out_a,
            "w_in_b": w_in_b, "v_g": v_g, "ws": ws, "bs": bs, "w_out_b": w_out_b}


def reference(x_prompt, x_sample, cache_k, cache_v, cache_logf, norm_g, w_in_a, b_f, q_g, k_g, w_out_a,
              w_in_b, v_g, ws, bs, w_out_b):
    xp, xs = x_prompt, x_sample
    kp_l, vp_l, lp_l, ks_l, vs_l, ls_l, sgu_l = [], [], [], [], [], [], []
    for i in range(DEPTH):
        hp = rmsnorm(xp, norm_g[i])
        hs = rmsnorm(xs, norm_g[i])
        j = i // N_MIXERS
        if i % N_MIXERS == 0:
            yp, kp, vp, lp = fox_prompt(hp, w_in_a[j], b_f[j], q_g[j], k_g[j], w_out_a[j])
            ys, ks_, vs_, ls_ = fox_sample(hs, cache_k[j], cache_v[j], cache_logf[j],
                                           w_in_a[j], b_f[j], q_g[j], k_g[j], w_out_a[j])
            kp_l.append(kp); vp_l.append(vp); lp_l.append(lp)
            ks_l.append(ks_); vs_l.append(vs_); ls_l.append(ls_)
        else:
            yp = gmlp_prompt(hp, w_in_b[j], v_g[j], ws[j], bs[j], w_out_b[j])
            ys, sv = gmlp_sample(hs, w_in_b[j], v_g[j], ws[j], bs[j], w_out_b[j])
            sgu_l.append(sv)
        xp = xp + yp
        xs = xs + ys
    k_prompt = jnp.stack(kp_l)
    v_prompt = jnp.stack(vp_l)
    logf_prompt = jnp.stack(lp_l)
    k_sample = jnp.stack(ks_l)
    v_sample = jnp.stack(vs_l)
    logf_sample = jnp.stack(ls_l)
    sgu_v_sample = jnp.stack(sgu_l)
    return (xp, xs, k_prompt, v_prompt, logf_prompt, k_sample, v_sample, logf_sample, sgu_v_sample)
```

```python
import numpy as np
from contextlib import ExitStack
import concourse.bass as bass
import concourse.mybir as mybir
from concourse.bass_utils import run_bass_kernel_spmd

F32 = mybir.dt.float32
BF16 = mybir.dt.bfloat16
ALU = mybir.AluOpType
AF = mybir.ActivationFunctionType

D = 1024
KC = 8
H = 8
HD = 128
EPS = 1e-6
NEG = -1e30
A_COLS = 4 * 1024 + 8
B_COLS = 3 * 2048
SCALE = HD ** -0.5
N_CORES = 8


class Op:
    __slots__ = ("eng", "fn", "waits", "signal", "is_dma", "sem", "val", "idx")


class Sched:
    ENGS = ("pe", "act", "dve", "pool", "sp")
    NDMA = {"pe": 8, "act": 8, "dve": 8, "pool": 40, "sp": 24}

    def __init__(self):
        self.streams = {e: [] for e in self.ENGS}
        self.last_w = {}
        self.readers = {}
        self.dma_count = {e: 0 for e in self.ENGS}

    def add(self, eng, fn, reads=(), writes=(), dma=False):
        op = Op()
        op.eng = eng; op.fn = fn; op.waits = []; op.signal = dma; op.is_dma = dma
        op.sem = None; op.val = None; op.idx = None
        deps = []
        for k in reads:
            w = self.last_w.get(k)
            if w is not None:
                deps.append(w)
        for k in writes:
            w = self.last_w.get(k)
            if w is not None:
                deps.append(w)
            deps.extend(self.readers.get(k, ()))
        seen = set()
        for d in deps:
            if id(d) in seen or d is op:
                continue
            seen.add(id(d))
            if (not d.is_dma) and d.eng == "pe" and eng == "pe" and not dma:
                continue
            d.signal = True
            op.waits.append(d)
        if dma:
            op.idx = self.dma_count[eng]
            self.dma_count[eng] += 1
        for k in writes:
            self.last_w[k] = op
            self.readers[k] = []
        for k in reads:
            if k not in writes:
                self.readers.setdefault(k, []).append(op)
        self.streams[eng].append(op)
        return op

    def emit(self, nc, es):
        sems = {e: es.enter_context(nc.semaphore("s_" + e)) for e in self.ENGS}
        dsems = {e: [es.enter_context(nc.semaphore("d_%s%d" % (e, i))) for i in range(self.NDMA[e])]
                 for e in self.ENGS if self.dma_count[e] > 0}
        for e in self.ENGS:
            n = 0
            for op in self.streams[e]:
                if op.is_dma:
                    op.sem = dsems[e][op.idx % self.NDMA[e]]
                    op.val = 16 * (op.idx // self.NDMA[e] + 1)
                elif op.signal:
                    n += 1
                    op.sem = sems[e]
                    op.val = n
        block = es.enter_context(nc.Block())

        def run(e, engine):
            waited = {}
            stream = self.streams[e]
            dma_ops = [op for op in stream if op.is_dma]
            for op in stream:
                need = [(d.sem, d.val) for d in op.waits]
                if op.is_dma and op.idx >= self.NDMA[e]:
                    prev = dma_ops[op.idx - self.NDMA[e]]
                    need.append((prev.sem, prev.val))
                for (s, v) in need:
                    if waited.get(s.name, 0) >= v:
                        continue
                    waited[s.name] = v
                    engine.wait_ge(s, v)
                ins = op.fn(engine)
                if op.is_dma:
                    ins.then_inc(op.sem, 16)
                elif op.signal:
                    ins.then_inc(op.sem, 1)
            if e == "sp":
                for ee in self.ENGS:
                    dl = [op for op in self.streams[ee] if op.is_dma]
                    for op in dl[-self.NDMA[ee]:]:
                        if waited.get(op.sem.name, 0) < op.val:
                            waited[op.sem.name] = op.val
                            engine.wait_ge(op.sem, op.val)

        block.tensor(lambda eng: run("pe", eng))
        block.scalar(lambda eng: run("act", eng))
        block.vector(lambda eng: run("dve", eng))
        block.gpsimd(lambda eng: run("pool", eng))
        block.sync(lambda eng: run("sp", eng))


def build(TP=2048, NS=4, PAST=4096, depth=4):
    NBP = TP // 128
    NBS = NS // 2
    NB = NBP + NBS
    T = NB * 128
    TS = NS * 64
    NPB = PAST // 128
    CB = min(8, NPB)
    NCH = NPB // CB
    NQC = TP // 512
    NA = (depth + 1) // 2
    NBL = depth // 2
    tchunks = [(i * 512, 512) for i in range(TP // 512)]
    s0 = TP
    while s0 < T:
        w = min(512, T - s0)
        tchunks.append((s0, w))
        s0 += w

    nc = bass.Bass("TRN2", target_bir_lowering=False)

    def din(name, shape):
        return nc.dram_tensor(name, list(shape), F32, kind="ExternalInput").ap()

    def dout(name, shape):
        return nc.dram_tensor(name, list(shape), F32, kind="ExternalOutput").ap()

    xp_d = din("xp", [TP, D]); xs_d = din("xs", [TS, D])
    ck_d = din("ckT", [NA, NS, H, HD, PAST]); cv_d = din("cvh", [NA, NS, H, 128, NPB, HD])
    clf_d = din("clf", [NA, NS, PAST, H])
    ng_d = din("norm_g", [depth, D]); wia_d = din("w_in_a_h", [NA, H, D, 512]); wf_d = din("w_f", [NA, D, 8])
    bf_d = din("b_f", [NA, H])
    qg_d = din("q_g", [NA, HD]); kg_d = din("k_g", [NA, HD]); woa_d = din("w_out_a", [NA, 1024, D])
    wib_d = din("w_in_b_g", [NBL, 8, D, 768]); vg_d = din("v_g", [NBL, 8 * 256]); ws_d = din("ws", [NBL, 8, 128, 128])
    bs_d = din("bs", [NBL, 8 * 128]); wob_d = din("w_out_b", [NBL, 2048, D])
    yp_d = dout("yp", [TP, D]); ys_d = dout("ys", [TS, D])
    kp_d = dout("kp", [NA, TP, H, HD]); vp_d = dout("vp", [NA, TP, H, HD]); lp_d = dout("lp", [NA, TP, H])
    ks_d = dout("ks", [NA, TS, H, HD]); vs_d = dout("vs", [NA, TS, H, HD]); ls_d = dout("ls", [NA, TS, H])
    sgu_d = dout("sgu", [NBL, TS, 2048])

    S = Sched()
    es = ExitStack()
    with es:
        def sb(name, shape, dt):
            return es.enter_context(nc.sbuf_tensor(name, list(shape), dt))

        x = sb("x", [128, NB, D], F32)
        hT = sb("hT", [128, KC, T], BF16)
        GT = sb("GT", [128, 2, T], BF16)
        QKT = sb("QKT", [128, 2, T], BF16)
        VN = sb("VN", [128, NB, 256], BF16)
        wr = [sb("wr%d" % i, [128, 4096], BF16) for i in range(2)]
        stg = [sb("stg%d" % i, [128, 384], F32) for i in range(3)]
        hbf = sb("hbf", [128, D], BF16)
        tmpf_all = sb("tmpf", [128, 1024], F32)
        tmpf = [tmpf_all[:, 0:512], tmpf_all[:, 512:1024]]
        qkb = [sb("qkb%d" % i, [128, 256], BF16) for i in range(5)]
        ident_bf = sb("ident_bf", [128, 128], BF16); ident_f = sb("ident_f", [128, 128], F32)
        ones_bf = sb("ones_bf", [128, 128], BF16); ones_f = sb("ones_f", [128, 128], F32)
        selbf = sb("selbf", [128, 128], BF16)
        caus_bf = sb("caus_bf", [128, 128], BF16)
        triU = sb("triU", [128, 128], F32); triS = sb("triS", [128, 128], F32); triSL = sb("triSL", [128, 128], F32)
        caus = sb("caus", [128, 128], F32)
        smask = [sb("smask%d" % r, [128, 64], F32) for r in range(2)]
        cpow = sb("cpow", [128, 32], F32)
        gn_bc = tmpf_all
        ss = sb("ss", [128, 32], F32); ssn = sb("ssn", [128, 32], F32); rstd = sb("rstd", [128, 32], F32)
        ss2 = [sb("ss2_%d" % i, [128, 2], F32) for i in range(5)]
        sn2 = [sb("sn2_%d" % i, [128, 2], F32) for i in range(5)]
        rs2 = [sb("rs2_%d" % i, [128, 2], F32) for i in range(5)]
        ARENA = 37 * 1024
        arena = sb("arena", [128, ARENA], mybir.dt.uint8)

        class Carver:
            def __init__(self):
                self.off = 0
            def take(self, shape, dt):
                esz = 4 if dt == F32 else 2
                n = 1
                for s_ in shape[1:]:
                    n *= s_
                nbytes = n * esz
                self.off = (self.off + 31) // 32 * 32
                assert self.off + nbytes <= ARENA, (self.off, nbytes)
                v = arena[:, self.off:self.off + nbytes].bitcast(dt)
                self.off += nbytes
                if len(shape) == 3:
                    v = v.rearrange("p (a b) -> p a b", a=shape[1])
                return v

        ca = Carver()
        Vc = [ca.take([128, CB, 128], BF16) for _ in range(2)]
        KTc = [ca.take([128, CB * 128], BF16) for _ in range(2)]
        PT = [ca.take([128, 512], BF16) for _ in range(2)]
        Fq = [ca.take([128, 512], F32) for _ in range(2)]
        FqS = ca.take([128, 128], F32)
        FqH = ca.take([128, 128], BF16)
        FqL = ca.take([128, 128], F32)
        rcg = ca.take([128, 512], F32)
        rcgS = ca.take([128, 64], F32)
        PTs = ca.take([128, 512], BF16)
        gscS = ca.take([128, 128 * max(NBS, 1)], BF16)
        Dg = [ca.take([128, 128], F32) for _ in range(2)]
        gsc = [ca.take([128, 512], BF16) for _ in range(2)]
        negG = [ca.take([128, H * NPB], F32) for _ in range(NS)]
        Lc = ca.take([128, NPB * H], F32)
        scn = [ca.take([128, max(NPB, NBP) * H], F32) for _ in range(2)]
        scd = ca.take([128, max(NPB, NBP) * H], F32)
        qkg_bc = ca.take([128, 256], F32)
        bf_bc = ca.take([128, 8], F32)
        Wf = ca.take([128, KC * 8], BF16)
        lg = [ca.take([128, NB * 8], F32) for _ in range(4)]
        fcum = ca.take([128, NB * 8], F32)
        negf = ca.take([128, NB * 8], F32)
        fox_end = ca.off
        cb = Carver()
        wsf = cb.take([128, 8, 128], F32)
        wsb = cb.take([128, 8, 128], BF16)
        WsTp = cb.take([128, 8, 128], BF16)
        WsTs = cb.take([128, 8, 128], BF16)
        vg_bc = cb.take([128, 2048], F32)
        bsf = cb.take([128, 1024], F32)
        bsh = cb.take([128, 1024], BF16)
        bsl = cb.take([128, 1024], F32)
        bsr = cb.take([128, 1024], BF16)
        bsrs = cb.take([128, 1024], BF16)

        pb = [es.enter_context(nc.psum_tensor("pb%d" % i, [128, 512], F32)) for i in (0, 1, 3, 4, 5, 6, 7)]
        b2 = es.enter_context(nc.psum_tensor("pb2", [128, 512], F32))
        pT = b2[:, :].bitcast(BF16)
        b0, b1, b3, b4, b5, b6, b7 = pb
        BK = {id(b0): "b0", id(b1): "b1", id(b3): "b3", id(b4): "b4", id(b5): "b5", id(b6): "b6",
              id(b7): "b7", id(b2): "b2"}

        def MM(out, lhsT, rhs, st, sp_, rd, wrk):
            S.add("pe", lambda e: e.matmul(out, lhsT=lhsT, rhs=rhs, start=st, stop=sp_), reads=rd, writes=wrk)

        def TR(out, in_, idn, rd, wrk):
            S.add("pe", lambda e: e.transpose(out, in_, idn), reads=rd, writes=wrk)

        def ACT(out, in_, func, rd, wrk, bias=None, scale=None, accum=None):
            kw = {}
            if bias is not None:
                kw["bias"] = bias
            if scale is not None:
                kw["scale"] = scale
            if accum is not None:
                kw["accum_out"] = accum
            S.add("act", lambda e: e.activation(out=out, in_=in_, func=func, **kw), reads=rd, writes=wrk)

        def TT(eng, out, in0, in1, op, rd, wrk):
            S.add(eng, lambda e: e.tensor_tensor(out=out, in0=in0, in1=in1, op=op), reads=rd, writes=wrk)

        def TS(eng, out, in0, s1, s2, op0, op1, rd, wrk):
            if s2 is None:
                S.add(eng, lambda e: e.tensor_scalar(out=out, in0=in0, scalar1=s1, scalar2=None, op0=op0),
                      reads=rd, writes=wrk)
            else:
                S.add(eng, lambda e: e.tensor_scalar(out=out, in0=in0, scalar1=s1, scalar2=s2, op0=op0, op1=op1),
                      reads=rd, writes=wrk)

        def STT(eng, out, in0, sc, in1, op0, op1, rd, wrk):
            S.add(eng, lambda e: e.scalar_tensor_tensor(out=out, in0=in0, scalar=sc, in1=in1, op0=op0, op1=op1),
                  reads=rd, writes=wrk)

        def CP(eng, out, in_, rd, wrk):
            S.add(eng, lambda e: e.tensor_copy(out=out, in_=in_), reads=rd, writes=wrk)

        def MS(eng, out, val, wrk):
            S.add(eng, lambda e: e.memset(out, val), writes=wrk)

        def ASEL(out, in_, pattern, cmp, fill, base, cm, key):
            S.add("pool", lambda e: e.affine_select(out=out, in_=in_, pattern=pattern, compare_op=cmp, fill=fill,
                                                    base=base, channel_multiplier=cm), reads=[key], writes=[key])

        def DMA(q, out, in_, rd, wrk):
            S.add(q, lambda e: e.dma_start(out=out, in_=in_), reads=rd, writes=wrk, dma=True)

        def xk(blk):
            return ("x", blk)

        def blks_of(t0, w):
            return list(range(t0 // 128, (t0 + w) // 128))

        MS("pool", ident_f[:], 1.0, ["ident_f"])
        ASEL(ident_f[:], ident_f[:], [[-1, 128]], ALU.is_equal, 0.0, 0, 1, "ident_f")
        CP("pool", ident_bf[:], ident_f[:], ["ident_f"], ["ident_bf"])
        MS("pool", ones_f[:], 1.0, ["ones_f"])
        MS("pool", ones_bf[:], 1.0, ["ones_bf"])
        MS("pool", cpow[:], -0.5, ["cpow"])
        MS("pool", selbf[0:64, :], 0.0, ["selbf"])
        MS("pool", selbf[0:1, :], 1.0, ["selbf"])
        MS("pool", selbf[32:33, :], 1.0, ["selbf"])
        MS("pool", triU[:], 1.0, ["triU"])
        ASEL(triU[:], triU[:], [[1, 128]], ALU.is_ge, 0.0, 0, -1, "triU")
        CP("pool", triS[:], triU[:], ["triU"], ["triS"])
        MS("pool", triS[0:64, 64:128], 0.0, ["triS"])
        MS("pool", triSL[:], 1.0, ["triSL"])
        ASEL(triSL[:], triSL[:], [[-1, 128]], ALU.is_gt, 0.0, 0, 1, "triSL")
        MS("pool", caus[:], 0.0, ["caus"])
        ASEL(caus[:], caus[:], [[1, 128]], ALU.is_ge, NEG, 0, -1, "caus")
        CP("pool", caus_bf[:], caus[:], ["caus"], ["caus_bf"])
        for r in range(2):
            MS("pool", smask[r][:], 0.0, [("smask", r)])
            ASEL(smask[r][:], smask[r][:], [[1, 64]], ALU.is_ge, NEG, 64 * r, -1, ("smask", r))
        ASEL(smask[1][:], smask[1][:], [[0, 64]], ALU.is_ge, NEG, -64, 1, ("smask", 1))

        xpv = xp_d.rearrange("(n p) d -> p n d", p=128)
        for n0 in range(0, NBP, 4):
            DMA("sp", x[:, n0:n0 + 4, :], xpv[:, n0:n0 + 4, :], [], [xk(b) for b in range(n0, n0 + 4)])
        xsv = xs_d.rearrange("(n p) d -> p n d", p=128)
        DMA("sp", x[:, NBP:NB, :], xsv, [], [xk(b) for b in range(NBP, NB)])

        wjobs = []
        for l_ in range(depth):
            j_ = l_ // 2
            if l_ % 2 == 0:
                for hp in range(4):
                    wjobs.append(("head", j_, 2 * hp)); wjobs.append(("head", j_, 2 * hp + 1))
                    wjobs.append(("outa", j_, hp))
            else:
                for g_ in range(8):
                    wjobs.append(("gB", j_, g_)); wjobs.append(("gA", j_, g_)); wjobs.append(("outb", j_, g_))
        wst = {"issued": 0, "used": 0}

        def w_emit(job, slot):
            kind, j_, n_ = job
            wk = [("w", slot)]
            v8 = wr[slot][:].rearrange("p (a b) -> p a b", a=KC)
            v2 = wr[slot][:, 0:2048].rearrange("p (a b) -> p a b", a=2)
            if kind == "head":
                DMA("pool", v8, wia_d[j_, n_].rearrange("(a p) c -> p a c", p=128), [], wk)
            elif kind == "outa":
                DMA("pool", v2, woa_d[j_, n_ * 256:(n_ + 1) * 256, :].rearrange("(a p) c -> p a c", p=128), [], wk)
            elif kind == "outb":
                DMA("pool", v2, wob_d[j_, n_ * 256:(n_ + 1) * 256, :].rearrange("(a p) c -> p a c", p=128), [], wk)
            elif kind == "gA":
                DMA("pool", v8, wib_d[j_, n_, :, 0:512].rearrange("(a p) c -> p a c", p=128), [], wk)
            elif kind == "gB":
                DMA("pool", v8[:, :, 0:256], wib_d[j_, n_, :, 512:768].rearrange("(a p) c -> p a c", p=128), [], wk)

        def w_issue_next():
            n_ = wst["issued"]
            if n_ < len(wjobs):
                wst["issued"] = n_ + 1
                w_emit(wjobs[n_], n_ % 2)

        def w_acquire(expect):
            n_ = wst["used"]
            assert wjobs[n_] == expect, (wjobs[n_], expect)
            wst["used"] = n_ + 1
            while wst["issued"] <= n_:
                w_issue_next()
            return n_ % 2

        def w_release():
            w_issue_next()

        w_issue_next()
        w_issue_next()

        def rmsnorm_hT(l):
            DMA("sp", gn_bc[:], ng_d[l:l + 1, :].partition_broadcast(128), [], [("tmpf", 0), ("tmpf", 1)])
            for blk in range(NB):
                ACT(hbf[:], x[:, blk, :], AF.Square, [xk(blk)], ["hbf", ("ss", blk)], accum=ss[:, blk:blk + 1])
            TS("dve", ssn[:, 0:NB], ss[:, 0:NB], 1.0 / D, EPS, ALU.mult, ALU.add,
               [("ss", b) for b in range(NB)], ["ssn"])
            TT("pool", rstd[:, 0:NB], ssn[:, 0:NB], cpow[:, 0:NB], ALU.pow, ["ssn", "cpow"], ["rstd"])
            for blk in range(NB):
                STT("dve", hbf[:], x[:, blk, :], rstd[:, blk:blk + 1], gn_bc[:], ALU.mult, ALU.mult,
                    [xk(blk), "rstd", ("tmpf", 0), ("tmpf", 1)], ["hbf"])
                for kc in range(KC):
                    TR(pT[:, kc * 128:(kc + 1) * 128], hbf[:, kc * 128:(kc + 1) * 128], ident_bf[:],
                       ["hbf", "ident_bf"], ["b2"])
                ACT(hT[:, :, blk * 128:(blk + 1) * 128], pT[:, :].rearrange("p (a b) -> p a b", a=KC), AF.Copy,
                    [], ["b2", ("hT", blk)])

        def out_round_gen(job):
            wi = w_acquire(job)
            return _out_round_body(wi)

        def _out_round_body(wi):
            wv = wr[wi][:, 0:2048].rearrange("p (a b) -> p a b", a=2)
            i = 0
            for blk in range(NB):
                for hf in range(2):
                    bank = (b5, b6)[i % 2]
                    i += 1
                    bk = BK[id(bank)]
                    for c in range(2):
                        MM(bank[:, :], GT[:, c, blk * 128:(blk + 1) * 128], wv[:, c, hf * 512:(hf + 1) * 512],
                           c == 0, c == 1, [("GT", c, blk), ("w", wi)], [bk])
                    TT("dve", x[:, blk, hf * 512:(hf + 1) * 512], bank[:, :], x[:, blk, hf * 512:(hf + 1) * 512],
                       ALU.add, [], [bk, xk(blk)])
                    yield
            w_release()

        pend_out = {"g": None}

        def step_out(n):
            g_ = pend_out["g"]
            if g_ is None:
                return
            for _ in range(n):
                try:
                    next(g_)
                except StopIteration:
                    pend_out["g"] = None
                    return

        def flush_out():
            step_out(10 ** 9)

        def scan_blocks(src_ps, bkey, nb, reverse, dst):
            a, b_ = scn[0], scn[1]
            av = a[:, 0:nb * 8]; bv = b_[:, 0:nb * 8]
            if nb == 1:
                MS("pool", dst, 0.0, ["scan_dst"])
                return
            if not reverse:
                MS("pool", av[:, 0:8], 0.0, ["scnA"])
                ACT(av[:, 8:nb * 8], src_ps[:, 0:(nb - 1) * 8], AF.Copy, [], [bkey, "scnA"])
            else:
                MS("pool", av[:, (nb - 1) * 8:nb * 8], 0.0, ["scnA"])
                ACT(av[:, 0:(nb - 1) * 8], src_ps[:, 8:nb * 8], AF.Copy, [], [bkey, "scnA"])
            cur, curk, oth, othk = av, "scnA", bv, "scnB"
            s_ = 1
            while s_ < nb:
                if not reverse:
                    TT("pool", oth[:, s_ * 8:nb * 8], cur[:, s_ * 8:nb * 8], cur[:, 0:(nb - s_) * 8], ALU.add,
                       [curk], [othk])
                    CP("pool", oth[:, 0:s_ * 8], cur[:, 0:s_ * 8], [curk], [othk])
                else:
                    TT("pool", oth[:, 0:(nb - s_) * 8], cur[:, 0:(nb - s_) * 8], cur[:, s_ * 8:nb * 8], ALU.add,
                       [curk], [othk])
                    CP("pool", oth[:, (nb - s_) * 8:nb * 8], cur[:, (nb - s_) * 8:nb * 8], [curk], [othk])
                cur, curk, oth, othk = oth, othk, cur, curk
                s_ *= 2
            CP("pool", dst, cur, [curk], ["scan_dst"])

        def fox_layer(l):
            j = l // 2
            for bi in range(NS):
                Lv = Lc[:].rearrange("p (a b) -> p a b", b=8)
                src = clf_d[j, bi].rearrange("(n p) h -> p n h", p=128)
                for q0 in range(0, NPB, 8):
                    q1 = min(NPB, q0 + 8)
                    DMA("sp", Lv[:, q0:q1, :], src[:, q0:q1, :], [], ["Lc"])
                MM(b7[:, 0:NPB * 8], triSL[:], Lc[:], True, True, ["triSL", "Lc"], ["b7"])
                MM(b6[:, 0:NPB * 8], ones_f[:], Lc[:], True, True, ["ones_f", "Lc"], ["b6"])
                scan_blocks(b6, "b6", NPB, True, scd[:, 0:NPB * 8])
                TT("dve", negG[bi][:].rearrange("p (h n) -> p n h", h=8),
                   b7[:, 0:NPB * 8].rearrange("p (n h) -> p n h", h=8),
                   scd[:, 0:NPB * 8].rearrange("p (n h) -> p n h", h=8), ALU.add, ["scan_dst"], ["b7", ("negG", bi)])

            rmsnorm_hT(l)
            DMA("sp", qkg_bc[:, 0:128], qg_d[j:j + 1, :].partition_broadcast(128), [], ["qkg"])
            DMA("sp", qkg_bc[:, 128:256], kg_d[j:j + 1, :].partition_broadcast(128), [], ["qkg"])
            DMA("sp", bf_bc[:], bf_d[j:j + 1, :].partition_broadcast(128), [], ["bf_bc"])
            DMA("pool", Wf[:].rearrange("p (a b) -> p a b", a=KC),
                wf_d[j].rearrange("(a p) c -> p a c", p=128), [], ["Wf"])
            Wfv = Wf[:].rearrange("p (a b) -> p a b", a=KC)
            for blk in range(NB):
                for kc in range(KC):
                    MM(b7[:, blk * 8:(blk + 1) * 8], hT[:, kc, blk * 128:(blk + 1) * 128], Wfv[:, kc, :],
                       kc == 0, kc == KC - 1, [("hT", blk), "Wf"], ["b7"])
            NF = NB * 8
            xb_, ab_, eb_, lb_ = lg[0], lg[1], lg[2], lg[3]
            TT("dve", xb_[:].rearrange("p (a b) -> p a b", b=8), b7[:, 0:NF].rearrange("p (a b) -> p a b", b=8),
               bf_bc[:].unsqueeze(1).to_broadcast([128, NB, 8]), ALU.add, ["bf_bc"], ["b7", "lg0"])
            ACT(ab_[:], xb_[:], AF.Abs, ["lg0"], ["lg1"])
            ACT(eb_[:], ab_[:], AF.Exp, ["lg1"], ["lg2"], scale=-1.0)
            ACT(lb_[:], eb_[:], AF.Ln, ["lg2"], ["lg3"], bias=1.0)
            TS("dve", ab_[:], xb_[:], 0.0, None, ALU.min, None, ["lg0"], ["lg1"])
            TT("dve", eb_[:], ab_[:], lb_[:], ALU.subtract, ["lg1", "lg3"], ["lg2"])
            logf = eb_
            DMA("sp", lp_d[j].rearrange("(n p) h -> p n h", p=128),
                logf[:, 0:NBP * 8].rearrange("p (a b) -> p a b", b=8), ["lg2"], [])
            DMA("sp", ls_d[j].rearrange("(n p) h -> p n h", p=128),
                logf[:, NBP * 8:NF].rearrange("p (a b) -> p a b", b=8), ["lg2"], [])
            MM(b7[:, 0:NBP * 8], triU[:], logf[:, 0:NBP * 8], True, True, ["triU", "lg2"], ["b7"])
            MM(b7[:, NBP * 8:NF], triS[:], logf[:, NBP * 8:NF], True, True, ["triS", "lg2"], ["b7"])
            MM(b6[:, 0:NBP * 8], ones_f[:], logf[:, 0:NBP * 8], True, True, ["ones_f", "lg2"], ["b6"])
            scan_blocks(b6, "b6", NBP, False, scd[:, 0:NBP * 8])
            TT("dve", fcum[:, 0:NBP * 8], b7[:, 0:NBP * 8], scd[:, 0:NBP * 8], ALU.add, ["scan_dst"], ["b7", "fcum"])
            ACT(fcum[:, NBP * 8:NF], b7[:, NBP * 8:NF], AF.Copy, [], ["b7", "fcum"])
            TS("dve", negf[:], fcum[:], -1.0, None, ALU.mult, None, ["fcum"], ["negf"])
            fcv = fcum[:].rearrange("p (a b) -> p a b", b=8)
            ngv = negf[:].rearrange("p (a b) -> p a b", b=8)
            for h in range(H):
                c_in = h % 2
                wi = w_acquire(("head", j, h))
                wv = wr[wi][:].rearrange("p (a b) -> p a b", a=KC)

                chunks = [(bi, ci) for bi in range(NS) for ci in range(NCH)]
                cstate = {"issued": 0}

                def issue_chunk():
                    n = cstate["issued"]
                    if n >= len(chunks):
                        return
                    cstate["issued"] = n + 1
                    bi_, ci_ = chunks[n]
                    ri_ = (h * len(chunks) + n) % 2
                    DMA("pool", KTc[ri_], ck_d[j, bi_, h, :, ci_ * CB * 128:(ci_ + 1) * CB * 128], [], [("KTc", ri_)])
                    DMA("pool", Vc[ri_], cv_d[j, bi_, h, :, ci_ * CB:(ci_ + 1) * CB, :], [], [("Vc", ri_)])

                issue_chunk()
                issue_chunk()

                RD = 5
                stg5 = [stg[0][:, :], stg[1][:, :], stg[2][:, :], tmpf[1][:, 0:384], hbf[:, :].bitcast(F32)[:, 0:384]]
                stg5k = [("stg", 0), ("stg", 1), ("stg", 2), ("tmpf", 1), "hbf"]

                def post_T(blk):
                    r5 = blk % RD
                    qb = qkb[r5]
                    TR(pT[:, 0:128], qb[:, 0:128], ident_bf[:], [("qkb", r5), "ident_bf"], ["b2"])
                    TR(pT[:, 128:256], qb[:, 128:256], ident_bf[:], [("qkb", r5), "ident_bf"], ["b2"])
                    ACT(QKT[:, :, blk * 128:(blk + 1) * 128], pT[:, 0:256].rearrange("p (a b) -> p a b", a=2),
                        AF.Copy, [], ["b2", ("QKT", blk)])

                def stageA(blk):
                    r5 = blk % RD
                    bank = (b0, b1, b7)[blk % 3]
                    bk = BK[id(bank)]
                    sg = stg5[r5]; sgk = stg5k[r5]
                    qb = qkb[r5]; qbk = ("qkb", r5)
                    s2, n2, r2 = ss2[r5], sn2[r5], rs2[r5]
                    for kc in range(KC):
                        MM(bank[:, 0:384], hT[:, kc, blk * 128:(blk + 1) * 128], wv[:, kc, 0:384],
                           kc == 0, kc == KC - 1, [("hT", blk), ("w", wi)], [bk])
                    step_out(2)
                    ACT(sg, bank[:, 0:384], AF.Copy, [], [bk, sgk])
                    ACT(qb[:, 0:128], sg[:, 0:128], AF.Square, [sgk], [qbk, ("ss2", r5)], accum=s2[:, 0:1])
                    ACT(qb[:, 128:256], sg[:, 128:256], AF.Square, [sgk], [qbk, ("ss2", r5)], accum=s2[:, 1:2])
                    TS("dve", n2[:], s2[:], 1.0 / HD, EPS, ALU.mult, ALU.add, [("ss2", r5)], [("sn2", r5)])
                    TT("pool", r2[:], n2[:], cpow[:, 0:2], ALU.pow, [("sn2", r5), "cpow"], [("rs2", r5)])

                def stageB(blk):
                    r5 = blk % RD
                    sg = stg5[r5]; sgk = stg5k[r5]
                    qb = qkb[r5]; qbk = ("qkb", r5)
                    r2 = rs2[r5]
                    STT("dve", qb[:, 0:128], sg[:, 0:128], r2[:, 0:1], qkg_bc[:, 0:128], ALU.mult, ALU.mult,
                        [sgk, ("rs2", r5), "qkg"], [qbk])
                    STT("dve", sg[:, 128:256], sg[:, 128:256], r2[:, 1:2], qkg_bc[:, 128:256], ALU.mult, ALU.mult,
                        [("rs2", r5), "qkg"], [sgk])
                    CP("pool", qb[:, 128:256], sg[:, 128:256], [sgk], [qbk])
                    CP("dve", VN[:, blk, 0:128], sg[:, 256:384], [sgk], [("VN", blk)])
                    if blk < NBP:
                        kd = kp_d[j, blk * 128:(blk + 1) * 128, h, :]
                        vd = vp_d[j, blk * 128:(blk + 1) * 128, h, :]
                    else:
                        kd = ks_d[j, (blk - NBP) * 128:(blk - NBP + 1) * 128, h, :]
                        vd = vs_d[j, (blk - NBP) * 128:(blk - NBP + 1) * 128, h, :]
                    DMA("sp", kd, sg[:, 128:256], [sgk], [])
                    DMA("sp", vd, sg[:, 256:384], [sgk], [])

                for it in range(NB + 3):
                    if it < NB:
                        stageA(it)
                    if 0 <= it - 1 < NB:
                        stageB(it - 1)
                    if 0 <= it - 3 < NB:
                        post_T(it - 3)
                flush_out()

                def gate_chunk(t0, w, gdst, gkey):
                    for kc in range(KC):
                        MM(b0[:, 0:w], wv[:, kc, 384:512], hT[:, kc, t0:t0 + w], kc == 0, kc == KC - 1,
                           [("hT", b) for b in blks_of(t0, w)] + [("w", wi)], ["b0"])
                    ACT(tmpf[0][:, 0:w], b0[:, 0:w], AF.Tanh, [], ["b0", ("tmpf", 0)], scale=0.5)
                    STT("dve", gdst[:, 0:w], tmpf[0][:, 0:w], 1.0, b0[:, 0:w], ALU.add, ALU.mult,
                        [("tmpf", 0)], ["b0", gkey])

                def finish(t0, w, c, gsrc, gkey, bo, bd, rc, rck, src_den=None, goff=0):
                    bok = BK[id(bo)]; bdk = BK[id(bd)]
                    S.add("dve", lambda e: e.reciprocal(out=rc[:, 0:w], in_=(bd[:, 0:w] if src_den is None else src_den)),
                          reads=(["dsum"] if src_den is not None else []), writes=[bdk, rck])
                    TT("pool", rc[:, 0:w], rc[:, 0:w], gsrc[:, goff:goff + w], ALU.mult, [gkey], [rck])
                    STT("dve", GT[:, c, t0:t0 + w], bo[:, 0:w], 0.5, rc[:, 0:w], ALU.mult, ALU.mult,
                        [rck], [bok] + [("GT", c, b) for b in blks_of(t0, w)])

                def fq_chunk(blk0, nblk, dst, dkey, scale=None):
                    for i in range(nblk):
                        dg = Dg[i % 2]
                        TS("dve", dg[:], ident_f[:], fcv[:, blk0 + i, h:h + 1], None, ALU.mult, None,
                           ["ident_f", "fcum"], [("Dg", i % 2)])
                        MM(b0[:, i * 128:(i + 1) * 128], ones_f[:], dg[:], True, True, ["ones_f", ("Dg", i % 2)], ["b0"])
                    ACT(dst[:, 0:nblk * 128], b0[:, 0:nblk * 128], AF.Copy, [], ["b0", dkey], scale=scale)

                rel = {"n": 0, "need": (1 if NQC > 0 else 0) + 1}

                def gates_done():
                    rel["n"] += 1
                    if rel["n"] == rel["need"]:
                        w_release()

                def prompt_gen():
                    pt_i = 0
                    bo, bd = b5, b6
                    bok, bdk = "b5", "b6"
                    if NQC > 0:
                        fq_chunk(0, 4, Fq[0], ("Fq", 0))
                    for qc in range(NQC):
                        t0 = qc * 512
                        fqt = Fq[qc % 2]; fqk = ("Fq", qc % 2)
                        gate_chunk(t0, 512, gsc[qc % 2], ("gsc", qc % 2))
                        if qc == NQC - 1:
                            gates_done()
                        yield
                        if qc + 1 < NQC:
                            fq_chunk(4 * (qc + 1), 4, Fq[(qc + 1) % 2], ("Fq", (qc + 1) % 2))
                        nkb = 4 * qc + 4
                        pend = None

                        def pv(kb, c0, pti):
                            ptile = PT[pti]
                            MM(bo[:, c0:512], VN[:, kb, 0:128], ptile[:, c0:512], kb == 0, kb == nkb - 1,
                               [("VN", kb), ("PT", pti)], [bok])
                            MM(bd[:, c0:512], ones_bf[:], ptile[:, c0:512], kb == 0, kb == nkb - 1,
                               ["ones_bf", ("PT", pti)], [bdk])

                        for kb in range(nkb):
                            c0 = max(0, kb - 4 * qc) * 128
                            st = (b3, b4)[kb % 2]; stk = BK[id(st)]
                            diag = kb >= 4 * qc
                            MM(st[:, c0:512], QKT[:, 1, kb * 128:(kb + 1) * 128], QKT[:, 0, t0 + c0:t0 + 512], True, not diag,
                               [("QKT", kb)] + [("QKT", b) for b in blks_of(t0 + c0, 512 - c0)], [stk])
                            if diag:
                                MM(st[:, c0:c0 + 128], ident_bf[:], caus_bf[:], False, True, ["ident_bf", "caus_bf"], [stk])
                            if pend is not None:
                                pv(*pend)
                            STT("dve", st[:, c0:512], st[:, c0:512], SCALE, fqt[:, c0:512], ALU.mult, ALU.add, [fqk], [stk])
                            pti = pt_i % 2
                            pt_i += 1
                            ACT(PT[pti][:, c0:512], st[:, c0:512], AF.Exp, ["negf"], [stk, ("PT", pti)],
                                bias=ngv[:, kb, h:h + 1])
                            pend = (kb, c0, pti)
                            yield
                        pv(*pend)
                        finish(t0, 512, c_in, gsc[qc % 2], ("gsc", qc % 2), bo, bd, rcg, "rcg")
                        yield

                def sample_gen():
                    cn = 0
                    bo, bd = b1, b7
                    bok, bdk = "b1", "b7"
                    st = b2; stk = "b2"
                    for jj_ in range(NBS):
                        gate_chunk(TP + 128 * jj_, 128, gscS[:, 128 * jj_:128 * (jj_ + 1)], "gscS")
                    gates_done()
                    yield
                    for bi in range(NS):
                        jj, r = bi // 2, bi % 2
                        sblk = NBP + jj
                        q0 = TP + 128 * jj + 64 * r
                        if r == 0:
                            fq_chunk(sblk, 1, FqS, "FqS", scale=1.0 / SCALE)
                            CP("dve", FqH[0:64, :], FqS[0:64, :], ["FqS"], ["FqH"])
                            TT("dve", FqL[32:64, :], FqS[32:64, :], FqH[32:64, :], ALU.subtract, ["FqS", "FqH"], ["FqL"])
                            CP("dve", FqH[32:64, :], FqL[32:64, :], ["FqL"], ["FqH"])
                            yield
                        ngb = negG[bi][:].rearrange("p (h n) -> p h n", h=8)
                        first = True
                        for ci in range(NCH):
                            ri = (h * len(chunks) + cn) % 2
                            cn += 1
                            for a in range(CB):
                                MM(st[:, a * 64:(a + 1) * 64], KTc[ri][:, a * 128:(a + 1) * 128], QKT[:, 0, q0:q0 + 64],
                                   a == 0, False, [("KTc", ri), ("QKT", sblk)], [stk])
                            MM(st[:, 0:CB * 64].rearrange("p (a b) -> p a b", a=CB), selbf[0:33, :],
                               FqH[0:33, 64 * r:64 * r + 64].unsqueeze(1).to_broadcast([33, CB, 64]), False, True,
                               ["selbf", "FqH"], [stk])
                            STT("dve", st[:, 0:CB * 64].rearrange("p (a b) -> p a b", a=CB),
                                st[:, 0:CB * 64].rearrange("p (a b) -> p a b", a=CB), SCALE,
                                ngb[:, h, ci * CB:(ci + 1) * CB].unsqueeze(2).to_broadcast([128, CB, 64]),
                                ALU.mult, ALU.add, [("negG", bi)], [stk])
                            ACT(PTs[:, 0:CB * 64], st[:, 0:CB * 64], AF.Exp, [], [stk, "PTs"])
                            yield
                            for a in range(CB):
                                MM(bo[:, 0:64], Vc[ri][:, a, :], PTs[:, a * 64:(a + 1) * 64], first, False,
                                   [("Vc", ri), "PTs"], [bok])
                                first = False
                            MM(bd[:, 0:CB * 64], ones_bf[:], PTs[:, 0:CB * 64], ci == 0, False,
                               ["ones_bf", "PTs"], [bdk])
                            issue_chunk()
                            yield
                        MM(st[:, 0:64], QKT[:, 1, sblk * 128:(sblk + 1) * 128], QKT[:, 0, q0:q0 + 64], True, False,
                           [("QKT", sblk)], [stk])
                        MM(st[:, 0:64], selbf[0:33, :], FqH[0:33, 64 * r:64 * r + 64], False, True, ["selbf", "FqH"], [stk])
                        STT("dve", st[:, 0:64], st[:, 0:64], SCALE, smask[r][:], ALU.mult, ALU.add, [("smask", r)], [stk])
                        ACT(PTs[:, 0:64], st[:, 0:64], AF.Exp, ["negf"], [stk, "PTs"], bias=ngv[:, sblk, h:h + 1])
                        yield
                        MM(bo[:, 0:64], VN[:, sblk, 0:128], PTs[:, 0:64], first, True, [("VN", sblk), "PTs"], [bok])
                        MM(bd[:, 0:64], ones_bf[:], PTs[:, 0:64], NCH == 0, True, ["ones_bf", "PTs"], [bdk])
                        if NCH > 0:
                            S.add("dve", (lambda bd_: (lambda e: e.tensor_reduce(
                                out=FqL[:, 0:64], in_=bd_[:, 0:CB * 64].rearrange("p (a q) -> p q a", a=CB),
                                axis=mybir.AxisListType.X, op=ALU.add)))(bd), reads=[], writes=[bdk, "dsum", "FqL"])
                            finish(q0, 64, c_in, gscS, "gscS", bo, bd, rcgS, "rcgS", src_den=FqL[:, 0:64], goff=128 * jj + 64 * r)
                        else:
                            finish(q0, 64, c_in, gscS, "gscS", bo, bd, rcgS, "rcgS", goff=128 * jj + 64 * r)
                        yield

                gens = [prompt_gen(), sample_gen()]
                quota = [1, 1]
                alive = [True, True]
                while any(alive):
                    for gi_ in range(2):
                        if not alive[gi_]:
                            continue
                        for _ in range(quota[gi_]):
                            try:
                                next(gens[gi_])
                            except StopIteration:
                                alive[gi_] = False
                                break

                if h % 2 == 1:
                    flush_out()
                    pend_out["g"] = out_round_gen(("outa", j, h // 2))
            flush_out()

        def gmlp_layer(l):
            j = l // 2
            DMA("sp", wsf, ws_d[j].rearrange("g t s -> t g s"), [], ["wsf"])
            MS("pool", wsf[0:64, :, 64:128], 0.0, ["wsf"])
            CP("pool", wsb, wsf, ["wsf"], ["wsb"])
            for g in range(8):
                TR(pT[:, g * 128:(g + 1) * 128], wsb[:, g, :], ident_bf[:], ["wsb", "ident_bf"], ["b2"])
            ACT(WsTp, pT[:, :].rearrange("p (a b) -> p a b", a=8), AF.Copy, [], ["b2", "WsTp"])
            MS("pool", wsf, 0.0, ["wsf"])
            DMA("sp", wsf[0:64, :, 0:64], ws_d[j, :, 0:64, 0:64].rearrange("g t s -> t g s"), [], ["wsf"])
            DMA("sp", wsf[64:128, :, 64:128], ws_d[j, :, 0:64, 0:64].rearrange("g t s -> t g s"), [], ["wsf"])
            CP("pool", wsb, wsf, ["wsf"], ["wsb"])
            for g in range(8):
                TR(pT[:, g * 128:(g + 1) * 128], wsb[:, g, :], ident_bf[:], ["wsb", "ident_bf"], ["b2"])
            ACT(WsTs, pT[:, :].rearrange("p (a b) -> p a b", a=8), AF.Copy, [], ["b2", "WsTs"])
            DMA("sp", bsf[0:2, :], bs_d[j:j + 1, :].partition_broadcast(2), [], ["bsf"])
            CP("pool", bsh[0:2, :], bsf[0:2, :], ["bsf"], ["bsh"])
            TT("pool", bsl[0:2, :], bsf[0:2, :], bsh[0:2, :], ALU.subtract, ["bsf", "bsh"], ["bsl"])
            CP("pool", bsr[0:2, :], bsl[0:2, :], ["bsl"], ["bsr"])
            CP("pool", bsr[0:1, :], bsh[0:1, :], ["bsh"], ["bsr"])
            bsrv = bsr[:].rearrange("p (g t) -> p g t", g=8)
            bsrsv = bsrs[:].rearrange("p (g t) -> p g t", g=8)
            CP("pool", bsrsv[0:2, :, 0:64], bsrv[0:2, :, 0:64], ["bsr"], ["bsrs"])
            CP("pool", bsrsv[0:2, :, 64:128], bsrv[0:2, :, 0:64], ["bsr"], ["bsrs"])
            DMA("sp", vg_bc[:], vg_d[j:j + 1, :].partition_broadcast(128), [], ["vg_bc"])
            rmsnorm_hT(l)

            for g in range(8):
                wb_ = w_acquire(("gB", j, g))
                wbv = wr[wb_][:].rearrange("p (a b) -> p a b", a=KC)
                def vA(blk):
                    r3 = blk % 3
                    bank = (b0, b1, b7)[r3]; bk = BK[id(bank)]
                    qb = qkb[r3]; qbk = ("qkb", r3)
                    s2, n2, r2 = ss2[r3], sn2[r3], rs2[r3]
                    for kc in range(KC):
                        MM(bank[:, 0:256], hT[:, kc, blk * 128:(blk + 1) * 128], wbv[:, kc, 0:256],
                           kc == 0, kc == KC - 1, [("hT", blk), ("w", wb_)], [bk])
                    step_out(2)
                    ACT(qb[:, 0:256], bank[:, 0:256], AF.Square, [], [bk, qbk, ("ss2", r3)], accum=s2[:, 0:1])
                    TS("dve", n2[:, 0:1], s2[:, 0:1], 1.0 / 256, EPS, ALU.mult, ALU.add, [("ss2", r3)], [("sn2", r3)])
                    TT("pool", r2[:, 0:1], n2[:, 0:1], cpow[:, 0:1], ALU.pow, [("sn2", r3), "cpow"], [("rs2", r3)])

                def vB(blk):
                    r3 = blk % 3
                    bank = (b0, b1, b7)[r3]; bk = BK[id(bank)]
                    sg = stg[blk % 2]; sgk = ("stg", blk % 2)
                    r2 = rs2[r3]
                    if blk < NBP:
                        STT("dve", VN[:, blk, :], bank[:, 0:256], r2[:, 0:1], vg_bc[:, g * 256:(g + 1) * 256],
                            ALU.mult, ALU.mult, [("rs2", r3), "vg_bc"], [bk, ("VN", blk)])
                    else:
                        STT("dve", sg[:, 0:256], bank[:, 0:256], r2[:, 0:1], vg_bc[:, g * 256:(g + 1) * 256],
                            ALU.mult, ALU.mult, [("rs2", r3), "vg_bc"], [bk, sgk])
                        CP("pool", VN[:, blk, :], sg[:, 0:256], [sgk], [("VN", blk)])
                        DMA("sp", sgu_d[j, (blk - NBP) * 128:(blk - NBP + 1) * 128, g * 256:(g + 1) * 256],
                            sg[:, 0:256], [sgk], [])

                for it in range(NB + 1):
                    if it < NB:
                        vA(it)
                    if it >= 1:
                        vB(it - 1)
                w_release()
                wa = w_acquire(("gA", j, g))
                wav = wr[wa][:].rearrange("p (a b) -> p a b", a=KC)
                ugi = 0
                for fc in range(2):
                    for ti, (t0, w) in enumerate(tchunks):
                        hk = [("hT", b) for b in blks_of(t0, w)]
                        bu, bg = ((b3, b4), (b7, b2))[ugi % 2]
                        buk, bgk = BK[id(bu)], BK[id(bg)]
                        tf = tmpf[ugi % 2]; tfk = ("tmpf", ugi % 2)
                        ugi += 1
                        for kc in range(KC):
                            MM(bg[:, 0:w], wav[:, kc, 256 + fc * 128:256 + (fc + 1) * 128], hT[:, kc, t0:t0 + w],
                               kc == 0, kc == KC - 1, hk + [("w", wa)], [bgk])
                        ACT(tf[:, 0:w], bg[:, 0:w], AF.Tanh, [], [bgk, tfk], scale=0.5)
                        for kc in range(KC):
                            MM(bu[:, 0:w], wav[:, kc, fc * 128:(fc + 1) * 128], hT[:, kc, t0:t0 + w],
                               kc == 0, kc == KC - 1, hk + [("w", wa)], [buk])
                        STT("dve", tf[:, 0:w], tf[:, 0:w], 1.0, bg[:, 0:w], ALU.add, ALU.mult, [], [bgk, tfk])
                        TT("dve", QKT[:, fc, t0:t0 + w], bu[:, 0:w], tf[:, 0:w], ALU.mult, [tfk],
                           [buk] + [("QKT", b) for b in blks_of(t0, w)])
                w_release()
                gi = 0
                for fc in range(2):
                    for (t0, w) in tchunks:
                        bank = (b5, b6)[gi % 2]; bk = BK[id(bank)]
                        gi += 1
                        for i, blk in enumerate(blks_of(t0, w)):
                            wst = WsTp if blk < NBP else WsTs
                            wsk = "WsTp" if blk < NBP else "WsTs"
                            brow = bsrv if blk < NBP else bsrsv
                            brk = "bsr" if blk < NBP else "bsrs"
                            MM(bank[:, i * 128:(i + 1) * 128], VN[:, blk, fc * 128:(fc + 1) * 128], wst[:, g, :],
                               True, False, [("VN", blk), wsk], [bk])
                            MM(bank[:, i * 128:(i + 1) * 128], ones_bf[0:2, :], brow[0:2, g, :], False, True,
                               ["ones_bf", brk], [bk])
                        STT("dve", GT[:, fc, t0:t0 + w], bank[:, 0:w], 0.5, QKT[:, fc, t0:t0 + w], ALU.mult, ALU.mult,
                            [("QKT", b) for b in blks_of(t0, w)], [bk] + [("GT", fc, b) for b in blks_of(t0, w)])
                flush_out()
                pend_out["g"] = out_round_gen(("outb", j, g))
            flush_out()

        def fence(to_fox):
            old = GM_KEYS if to_fox else FOX_KEYS
            new = FOX_KEYS if to_fox else GM_KEYS
            S.add("pool", lambda e: e.memset(ss[:, 31:32], 0.0), reads=[], writes=list(old) + list(new) + ["fence"])

        FOX_KEYS = ([("Vc", i) for i in range(2)] + [("KTc", i) for i in range(2)] +
                    [("PT", i) for i in range(2)] + [("Fq", i) for i in range(2)] + ["FqS", "FqH", "FqL", "rcg", "rcgS", "PTs", "gscS", "dsum"] +
                    [("Dg", i) for i in range(2)] + [("gsc", i) for i in range(2)] + [("negG", i) for i in range(NS)] +
                    ["Lc", "scnA", "scnB", "qkg", "bf_bc", "Wf", "lg0", "lg1", "lg2", "lg3", "fcum", "negf", "scan_dst"])
        GM_KEYS = ["wsf", "wsb", "WsTp", "WsTs", "vg_bc", "bsf", "bsh", "bsl", "bsr", "bsrs"]

        for l in range(depth):
            if l % 2 == 0:
                if l > 0:
                    fence(True)
                fox_layer(l)
            else:
                fence(False)
                gmlp_layer(l)

        ypv = yp_d.rearrange("(n p) d -> p n d", p=128)
        for n0 in range(0, NBP, 4):
            DMA("sp", ypv[:, n0:n0 + 4, :], x[:, n0:n0 + 4, :], [xk(b) for b in range(n0, n0 + 4)], [])
        DMA("sp", ys_d.rearrange("(n p) d -> p n d", p=128), x[:, NBP:NB, :], [xk(b) for b in range(NBP, NB)], [])

        S.emit(nc, es)
    return nc


_CACHE = {}


def kernel(x_prompt, x_sample, cache_k, cache_v, cache_logf, norm_g, w_in_a, b_f, q_g, k_g, w_out_a,
           w_in_b, v_g, ws, bs, w_out_b):
    f = lambda a: np.ascontiguousarray(np.asarray(a, dtype=np.float32))
    B, TP, _ = x_prompt.shape
    DB, DS, _ = x_sample.shape
    PAST = cache_k.shape[2]
    depth = norm_g.shape[0]
    n = N_CORES
    NS = DB // n
    key = (TP, NS, PAST, depth)
    if key not in _CACHE:
        _CACHE[key] = build(TP=TP, NS=NS, PAST=PAST, depth=depth)
    nc = _CACHE[key]
    NA = (depth + 1) // 2
    NBL = depth // 2
    wia = np.asarray(w_in_a, dtype=np.float32)
    wia_h = f(wia[:, :, :4096].reshape(NA, D, 4, H, HD).transpose(0, 3, 1, 2, 4).reshape(NA, H, D, 512))
    wib = np.asarray(w_in_b, dtype=np.float32)
    wib4 = wib.reshape(NBL, D, 3, 8, 256)
    wib_g = f(np.stack([wib4[:, :, 0], wib4[:, :, 2], wib4[:, :, 1]], axis=2).transpose(0, 3, 1, 2, 4).reshape(NBL, 8, D, 768))
    shared = {"norm_g": f(norm_g), "w_in_a_h": wia_h, "w_f": f(wia[:, :, 4096:4104]), "b_f": f(b_f), "q_g": f(q_g), "k_g": f(k_g),
              "w_out_a": f(w_out_a), "w_in_b_g": wib_g, "v_g": f(np.asarray(v_g).reshape(NBL, -1)),
              "ws": f(ws), "bs": f(np.asarray(bs).reshape(NBL, -1)), "w_out_b": f(w_out_b)}
    in_maps = []
    for c in range(n):
        m = dict(shared)
        m["xp"] = f(x_prompt[c])
        m["xs"] = f(np.asarray(x_sample[c * NS:(c + 1) * NS]).reshape(NS * DS, -1))
        m["ckT"] = f(np.asarray(cache_k[:, c * NS:(c + 1) * NS]).transpose(0, 1, 3, 4, 2))
        cvc = np.asarray(cache_v[:, c * NS:(c + 1) * NS])
        m["cvh"] = f(cvc.reshape(NA, NS, PAST // 128, 128, H, HD).transpose(0, 1, 4, 3, 2, 5))
        m["clf"] = f(cache_logf[:, c * NS:(c + 1) * NS])
        in_maps.append(m)
    res = run_bass_kernel_spmd(nc, in_maps, core_ids=list(range(n)))
    R = res.results
    y_prompt = np.stack([R[c]["yp"] for c in range(n)], 0)
    y_sample = np.concatenate([R[c]["ys"].reshape(NS, DS, D) for c in range(n)], 0)
    k_prompt = np.stack([R[c]["kp"] for c in range(n)], 1)
    v_prompt = np.stack([R[c]["vp"] for c in range(n)], 1)
    logf_prompt = np.stack([R[c]["lp"] for c in range(n)], 1)
    k_sample = np.concatenate([R[c]["ks"].reshape(NA, NS, DS, H, HD) for c in range(n)], 1)
    v_sample = np.concatenate([R[c]["vs"].reshape(NA, NS, DS, H, HD) for c in range(n)], 1)
    logf_sample = np.concatenate([R[c]["ls"].reshape(NA, NS, DS, H) for c in range(n)], 1)
    sgu = np.concatenate([R[c]["sgu"].reshape(NBL, NS, DS, 2048) for c in range(n)], 1)
    return (y_prompt, y_sample, k_prompt, v_prompt, logf_prompt, k_sample, v_sample, logf_sample, sgu)
```

```python
import numpy as np
from contextlib import ExitStack
import concourse.bass as bass
import concourse.mybir as mybir
from concourse.bass_utils import run_bass_kernel_spmd

F32 = mybir.dt.float32
BF16 = mybir.dt.bfloat16
ALU = mybir.AluOpType
AF = mybir.ActivationFunctionType

D = 1024
KC = 8
H = 8
HD = 128
EPS = 1e-6
NEG = -1e30
A_COLS = 4 * 1024 + 8
B_COLS = 3 * 2048
SCALE = HD ** -0.5
N_CORES = 8


class Op:
    __slots__ = ("eng", "fn", "waits", "signal", "is_dma", "sem", "val", "idx")


class Sched:
    ENGS = ("pe", "act", "dve", "pool", "sp")
    NDMA = {"pe": 8, "act": 8, "dve": 8, "pool": 40, "sp": 24}

    def __init__(self):
        self.streams = {e: [] for e in self.ENGS}
        self.last_w = {}
        self.readers = {}
        self.dma_count = {e: 0 for e in self.ENGS}

    def add(self, eng, fn, reads=(), writes=(), dma=False):
        op = Op()
        op.eng = eng; op.fn = fn; op.waits = []; op.signal = dma; op.is_dma = dma
        op.sem = None; op.val = None; op.idx = None
        deps = []
        for k in reads:
            w = self.last_w.get(k)
            if w is not None:
                deps.append(w)
        for k in writes:
            w = self.last_w.get(k)
            if w is not None:
                deps.append(w)
            deps.extend(self.readers.get(k, ()))
        seen = set()
        for d in deps:
            if id(d) in seen or d is op:
                continue
            seen.add(id(d))
            if (not d.is_dma) and d.eng == "pe" and eng == "pe" and not dma:
                continue
            d.signal = True
            op.waits.append(d)
        if dma:
            op.idx = self.dma_count[eng]
            self.dma_count[eng] += 1
        for k in writes:
            self.last_w[k] = op
            self.readers[k] = []
        for k in reads:
            if k not in writes:
                self.readers.setdefault(k, []).append(op)
        self.streams[eng].append(op)
        return op

    def emit(self, nc, es):
        sems = {e: es.enter_context(nc.semaphore("s_" + e)) for e in self.ENGS}
        dsems = {e: [es.enter_context(nc.semaphore("d_%s%d" % (e, i))) for i in range(self.NDMA[e])]
                 for e in self.ENGS if self.dma_count[e] > 0}
        for e in self.ENGS:
            n = 0
            for op in self.streams[e]:
                if op.is_dma:
                    op.sem = dsems[e][op.idx % self.NDMA[e]]
                    op.val = 16 * (op.idx // self.NDMA[e] + 1)
                elif op.signal:
                    n += 1
                    op.sem = sems[e]
                    op.val = n
        block = es.enter_context(nc.Block())

        def run(e, engine):
            waited = {}
            stream = self.streams[e]
            dma_ops = [op for op in stream if op.is_dma]
            for op in stream:
                need = [(d.sem, d.val) for d in op.waits]
                if op.is_dma and op.idx >= self.NDMA[e]:
                    prev = dma_ops[op.idx - self.NDMA[e]]
                    need.append((prev.sem, prev.val))
                for (s, v) in need:
                    if waited.get(s.name, 0) >= v:
                        continue
                    waited[s.name] = v
                    engine.wait_ge(s, v)
                ins = op.fn(engine)
                if op.is_dma:
                    ins.then_inc(op.sem, 16)
                elif op.signal:
                    ins.then_inc(op.sem, 1)
            if e == "sp":
                for ee in self.ENGS:
                    dl = [op for op in self.streams[ee] if op.is_dma]
                    for op in dl[-self.NDMA[ee]:]:
                        if waited.get(op.sem.name, 0) < op.val:
                            waited[op.sem.name] = op.val
                            engine.wait_ge(op.sem, op.val)

        block.tensor(lambda eng: run("pe", eng))
        block.scalar(lambda eng: run("act", eng))
        block.vector(lambda eng: run("dve", eng))
        block.gpsimd(lambda eng: run("pool", eng))
        block.sync(lambda eng: run("sp", eng))


def build(TP=2048, NS=4, PAST=4096, depth=4):
    NBP = TP // 128
    NBS = NS // 2
    NB = NBP + NBS
    T = NB * 128
    TS = NS * 64
    NPB = PAST // 128
    CB = min(8, NPB)
    NCH = NPB // CB
    NQC = TP // 512
    NA = (depth + 1) // 2
    NBL = depth // 2
    tchunks = [(i * 512, 512) for i in range(TP // 512)]
    s0 = TP
    while s0 < T:
        w = min(512, T - s0)
        tchunks.append((s0, w))
        s0 += w

    nc = bass.Bass("TRN2", target_bir_lowering=False)

    def din(name, shape):
        return nc.dram_tensor(name, list(shape), F32, kind="ExternalInput").ap()

    def dout(name, shape):
        return nc.dram_tensor(name, list(shape), F32, kind="ExternalOutput").ap()

    xp_d = din("xp", [TP, D]); xs_d = din("xs", [TS, D])
    ck_d = din("ckT", [NA, NS, H, HD, PAST]); cv_d = din("cvh", [NA, NS, H, 128, NPB, HD])
    clf_d = din("clf", [NA, NS, PAST, H])
    ng_d = din("norm_g", [depth, D]); wia_d = din("w_in_a_h", [NA, H, D, 512]); wf_d = din("w_f", [NA, D, 8])
    bf_d = din("b_f", [NA, H])
    qg_d = din("q_g", [NA, HD]); kg_d = din("k_g", [NA, HD]); woa_d = din("w_out_a", [NA, 1024, D])
    wib_d = din("w_in_b_g", [NBL, 8, D, 768]); vg_d = din("v_g", [NBL, 8 * 256]); ws_d = din("ws", [NBL, 8, 128, 128])
    bs_d = din("bs", [NBL, 8 * 128]); wob_d = din("w_out_b", [NBL, 2048, D])
    yp_d = dout("yp", [TP, D]); ys_d = dout("ys", [TS, D])
    kp_d = dout("kp", [NA, TP, H, HD]); vp_d = dout("vp", [NA, TP, H, HD]); lp_d = dout("lp", [NA, TP, H])
    ks_d = dout("ks", [NA, TS, H, HD]); vs_d = dout("vs", [NA, TS, H, HD]); ls_d = dout("ls", [NA, TS, H])
    sgu_d = dout("sgu", [NBL, TS, 2048])

    S = Sched()
    es = ExitStack()
    with es:
        def sb(name, shape, dt):
            return es.enter_context(nc.sbuf_tensor(name, list(shape), dt))

        x = sb("x", [128, NB, D], F32)
        hT = sb("hT", [128, KC, T], BF16)
        GT = sb("GT", [128, 2, T], BF16)
        QKT = sb("QKT", [128, 2, T], BF16)
        VN = sb("VN", [128, NB, 256], BF16)
        wr = [sb("wr%d" % i, [128, 4096], BF16) for i in range(2)]
        stg = [sb("stg%d" % i, [128, 384], F32) for i in range(3)]
        hbf = sb("hbf", [128, D], BF16)
        tmpf_all = sb("tmpf", [128, 1024], F32)
        tmpf = [tmpf_all[:, 0:512], tmpf_all[:, 512:1024]]
        qkb = [sb("qkb%d" % i, [128, 256], BF16) for i in range(5)]
        ident_bf = sb("ident_bf", [128, 128], BF16); ident_f = sb("ident_f", [128, 128], F32)
        ones_bf = sb("ones_bf", [128, 128], BF16); ones_f = sb("ones_f", [128, 128], F32)
        selbf = sb("selbf", [128, 128], BF16)
        caus_bf = sb("caus_bf", [128, 128], BF16)
        triU = sb("triU", [128, 128], F32); triS = sb("triS", [128, 128], F32); triSL = sb("triSL", [128, 128], F32)
        caus = sb("caus", [128, 128], F32)
        smask = [sb("smask%d" % r, [128, 64], F32) for r in range(2)]
        cpow = sb("cpow", [128, 32], F32)
        gn_bc = tmpf_all
        ss = sb("ss", [128, 32], F32); ssn = sb("ssn", [128, 32], F32); rstd = sb("rstd", [128, 32], F32)
        ss2 = [sb("ss2_%d" % i, [128, 2], F32) for i in range(5)]
        sn2 = [sb("sn2_%d" % i, [128, 2], F32) for i in range(5)]
        rs2 = [sb("rs2_%d" % i, [128, 2], F32) for i in range(5)]
        ARENA = 37 * 1024
        arena = sb("arena", [128, ARENA], mybir.dt.uint8)

        class Carver:
            def __init__(self):
                self.off = 0
            def take(self, shape, dt):
                esz = 4 if dt == F32 else 2
                n = 1
                for s_ in shape[1:]:
                    n *= s_
                nbytes = n * esz
                self.off = (self.off + 31) // 32 * 32
                assert self.off + nbytes <= ARENA, (self.off, nbytes)
                v = arena[:, self.off:self.off + nbytes].bitcast(dt)
                self.off += nbytes
                if len(shape) == 3:
                    v = v.rearrange("p (a b) -> p a b", a=shape[1])
                return v

        ca = Carver()
        Vc = [ca.take([128, CB, 128], BF16) for _ in range(2)]
        KTc = [ca.take([128, CB * 128], BF16) for _ in range(2)]
        PT = [ca.take([128, 512], BF16) for _ in range(2)]
        Fq = [ca.take([128, 512], F32) for _ in range(2)]
        FqS = ca.take([128, 128], F32)
        FqH = ca.take([128, 128], BF16)
        FqL = ca.take([128, 128], F32)
        rcg = ca.take([128, 512], F32)
        rcgS = ca.take([128, 64], F32)
        PTs = ca.take([128, 512], BF16)
        gscS = ca.take([128, 128 * max(NBS, 1)], BF16)
        Dg = [ca.take([128, 128], F32) for _ in range(2)]
        gsc = [ca.take([128, 512], BF16) for _ in range(2)]
        negG = [ca.take([128, H * NPB], F32) for _ in range(NS)]
        Lc = ca.take([128, NPB * H], F32)
        scn = [ca.take([128, max(NPB, NBP) * H], F32) for _ in range(2)]
        scd = ca.take([128, max(NPB, NBP) * H], F32)
        qkg_bc = ca.take([128, 256], F32)
        bf_bc = ca.take([128, 8], F32)
        Wf = ca.take([128, KC * 8], BF16)
        lg = [ca.take([128, NB * 8], F32) for _ in range(4)]
        fcum = ca.take([128, NB * 8], F32)
        negf = ca.take([128, NB * 8], F32)
        fox_end = ca.off
        cb = Carver()
        wsf = cb.take([128, 8, 128], F32)
        wsb = cb.take([128, 8, 128], BF16)
        WsTp = cb.take([128, 8, 128], BF16)
        WsTs = cb.take([128, 8, 128], BF16)
        vg_bc = cb.take([128, 2048], F32)
        bsf = cb.take([128, 1024], F32)
        bsh = cb.take([128, 1024], BF16)
        bsl = cb.take([128, 1024], F32)
        bsr = cb.take([128, 1024], BF16)
        bsrs = cb.take([128, 1024], BF16)

        pb = [es.enter_context(nc.psum_tensor("pb%d" % i, [128, 512], F32)) for i in (0, 1, 3, 4, 5, 6, 7)]
        b2 = es.enter_context(nc.psum_tensor("pb2", [128, 512], F32))
        pT = b2[:, :].bitcast(BF16)
        b0, b1, b3, b4, b5, b6, b7 = pb
        BK = {id(b0): "b0", id(b1): "b1", id(b3): "b3", id(b4): "b4", id(b5): "b5", id(b6): "b6",
              id(b7): "b7", id(b2): "b2"}

        def MM(out, lhsT, rhs, st, sp_, rd, wrk):
            S.add("pe", lambda e: e.matmul(out, lhsT=lhsT, rhs=rhs, start=st, stop=sp_), reads=rd, writes=wrk)

        def TR(out, in_, idn, rd, wrk):
            S.add("pe", lambda e: e.transpose(out, in_, idn), reads=rd, writes=wrk)

        def ACT(out, in_, func, rd, wrk, bias=None, scale=None, accum=None):
            kw = {}
            if bias is not None:
                kw["bias"] = bias
            if scale is not None:
                kw["scale"] = scale
            if accum is not None:
                kw["accum_out"] = accum
            S.add("act", lambda e: e.activation(out=out, in_=in_, func=func, **kw), reads=rd, writes=wrk)

        def TT(eng, out, in0, in1, op, rd, wrk):
            S.add(eng, lambda e: e.tensor_tensor(out=out, in0=in0, in1=in1, op=op), reads=rd, writes=wrk)

        def TS(eng, out, in0, s1, s2, op0, op1, rd, wrk):
            if s2 is None:
                S.add(eng, lambda e: e.tensor_scalar(out=out, in0=in0, scalar1=s1, scalar2=None, op0=op0),
                      reads=rd, writes=wrk)
            else:
                S.add(eng, lambda e: e.tensor_scalar(out=out, in0=in0, scalar1=s1, scalar2=s2, op0=op0, op1=op1),
                      reads=rd, writes=wrk)

        def STT(eng, out, in0, sc, in1, op0, op1, rd, wrk):
            S.add(eng, lambda e: e.scalar_tensor_tensor(out=out, in0=in0, scalar=sc, in1=in1, op0=op0, op1=op1),
                  reads=rd, writes=wrk)

        def CP(eng, out, in_, rd, wrk):
            S.add(eng, lambda e: e.tensor_copy(out=out, in_=in_), reads=rd, writes=wrk)

        def MS(eng, out, val, wrk):
            S.add(eng, lambda e: e.memset(out, val), writes=wrk)

        def ASEL(out, in_, pattern, cmp, fill, base, cm, key):
            S.add("pool", lambda e: e.affine_select(out=out, in_=in_, pattern=pattern, compare_op=cmp, fill=fill,
                                                    base=base, channel_multiplier=cm), reads=[key], writes=[key])

        def DMA(q, out, in_, rd, wrk):
            S.add(q, lambda e: e.dma_start(out=out, in_=in_), reads=rd, writes=wrk, dma=True)

        def xk(blk):
            return ("x", blk)

        def blks_of(t0, w):
            return list(range(t0 // 128, (t0 + w) // 128))

        MS("pool", ident_f[:], 1.0, ["ident_f"])
        ASEL(ident_f[:], ident_f[:], [[-1, 128]], ALU.is_equal, 0.0, 0, 1, "ident_f")
        CP("pool", ident_bf[:], ident_f[:], ["ident_f"], ["ident_bf"])
        MS("pool", ones_f[:], 1.0, ["ones_f"])
        MS("pool", ones_bf[:], 1.0, ["ones_bf"])
        MS("pool", cpow[:], -0.5, ["cpow"])
        MS("pool", selbf[0:64, :], 0.0, ["selbf"])
        MS("pool", selbf[0:1, :], 1.0, ["selbf"])
        MS("pool", selbf[32:33, :], 1.0, ["selbf"])
        MS("pool", triU[:], 1.0, ["triU"])
        ASEL(triU[:], triU[:], [[1, 128]], ALU.is_ge, 0.0, 0, -1, "triU")
        CP("pool", triS[:], triU[:], ["triU"], ["triS"])
        MS("pool", triS[0:64, 64:128], 0.0, ["triS"])
        MS("pool", triSL[:], 1.0, ["triSL"])
        ASEL(triSL[:], triSL[:], [[-1, 128]], ALU.is_gt, 0.0, 0, 1, "triSL")
        MS("pool", caus[:], 0.0, ["caus"])
        ASEL(caus[:], caus[:], [[1, 128]], ALU.is_ge, NEG, 0, -1, "caus")
        CP("pool", caus_bf[:], caus[:], ["caus"], ["caus_bf"])
        for r in range(2):
            MS("pool", smask[r][:], 0.0, [("smask", r)])
            ASEL(smask[r][:], smask[r][:], [[1, 64]], ALU.is_ge, NEG, 64 * r, -1, ("smask", r))
        ASEL(smask[1][:], smask[1][:], [[0, 64]], ALU.is_ge, NEG, -64, 1, ("smask", 1))

        xpv = xp_d.rearrange("(n p) d -> p n d", p=128)
        for n0 in range(0, NBP, 4):
            DMA("sp", x[:, n0:n0 + 4, :], xpv[:, n0:n0 + 4, :], [], [xk(b) for b in range(n0, n0 + 4)])
        xsv = xs_d.rearrange("(n p) d -> p n d", p=128)
        DMA("sp", x[:, NBP:NB, :], xsv, [], [xk(b) for b in range(NBP, NB)])

        wjobs = []
        for l_ in range(depth):
            j_ = l_ // 2
            if l_ % 2 == 0:
                for hp in range(4):
                    wjobs.append(("head", j_, 2 * hp)); wjobs.append(("head", j_, 2 * hp + 1))
                    wjobs.append(("outa", j_, hp))
            else:
                for g_ in range(8):
                    wjobs.append(("gB", j_, g_)); wjobs.append(("gA", j_, g_)); wjobs.append(("outb", j_, g_))
        wst = {"issued": 0, "used": 0}

        def w_emit(job, slot):
            kind, j_, n_ = job
            wk = [("w", slot)]
            v8 = wr[slot][:].rearrange("p (a b) -> p a b", a=KC)
            v2 = wr[slot][:, 0:2048].rearrange("p (a b) -> p a b", a=2)
            if kind == "head":
                DMA("pool", v8, wia_d[j_, n_].rearrange("(a p) c -> p a c", p=128), [], wk)
            elif kind == "outa":
                DMA("pool", v2, woa_d[j_, n_ * 256:(n_ + 1) * 256, :].rearrange("(a p) c -> p a c", p=128), [], wk)
            elif kind == "outb":
                DMA("pool", v2, wob_d[j_, n_ * 256:(n_ + 1) * 256, :].rearrange("(a p) c -> p a c", p=128), [], wk)
            elif kind == "gA":
                DMA("pool", v8, wib_d[j_, n_, :, 0:512].rearrange("(a p) c -> p a c", p=128), [], wk)
            elif kind == "gB":
                DMA("pool", v8[:, :, 0:256], wib_d[j_, n_, :, 512:768].rearrange("(a p) c -> p a c", p=128), [], wk)

        def w_issue_next():
            n_ = wst["issued"]
            if n_ < len(wjobs):
                wst["issued"] = n_ + 1
                w_emit(wjobs[n_], n_ % 2)

        def w_acquire(expect):
            n_ = wst["used"]
            assert wjobs[n_] == expect, (wjobs[n_], expect)
            wst["used"] = n_ + 1
            while wst["issued"] <= n_:
                w_issue_next()
            return n_ % 2

        def w_release():
            w_issue_next()

        w_issue_next()
        w_issue_next()

        def rmsnorm_hT(l):
            DMA("sp", gn_bc[:], ng_d[l:l + 1, :].partition_broadcast(128), [], [("tmpf", 0), ("tmpf", 1)])
            for blk in range(NB):
                ACT(hbf[:], x[:, blk, :], AF.Square, [xk(blk)], ["hbf", ("ss", blk)], accum=ss[:, blk:blk + 1])
            TS("dve", ssn[:, 0:NB], ss[:, 0:NB], 1.0 / D, EPS, ALU.mult, ALU.add,
               [("ss", b) for b in range(NB)], ["ssn"])
            TT("pool", rstd[:, 0:NB], ssn[:, 0:NB], cpow[:, 0:NB], ALU.pow, ["ssn", "cpow"], ["rstd"])
            for blk in range(NB):
                STT("dve", hbf[:], x[:, blk, :], rstd[:, blk:blk + 1], gn_bc[:], ALU.mult, ALU.mult,
                    [xk(blk), "rstd", ("tmpf", 0), ("tmpf", 1)], ["hbf"])
                for kc in range(KC):
                    TR(pT[:, kc * 128:(kc + 1) * 128], hbf[:, kc * 128:(kc + 1) * 128], ident_bf[:],
                       ["hbf", "ident_bf"], ["b2"])
                ACT(hT[:, :, blk * 128:(blk + 1) * 128], pT[:, :].rearrange("p (a b) -> p a b", a=KC), AF.Copy,
                    [], ["b2", ("hT", blk)])

        def out_round_gen(job):
            wi = w_acquire(job)
            return _out_round_body(wi)

        def _out_round_body(wi):
            wv = wr[wi][:, 0:2048].rearrange("p (a b) -> p a b", a=2)
            i = 0
            for blk in range(NB):
                for hf in range(2):
                    bank = (b5, b6)[i % 2]
                    i += 1
                    bk = BK[id(bank)]
                    for c in range(2):
                        MM(bank[:, :], GT[:, c, blk * 128:(blk + 1) * 128], wv[:, c, hf * 512:(hf + 1) * 512],
                           c == 0, c == 1, [("GT", c, blk), ("w", wi)], [bk])
                    TT("dve", x[:, blk, hf * 512:(hf + 1) * 512], bank[:, :], x[:, blk, hf * 512:(hf + 1) * 512],
                       ALU.add, [], [bk, xk(blk)])
                    yield
            w_release()

        pend_out = {"g": None}

        def step_out(n):
            g_ = pend_out["g"]
            if g_ is None:
                return
            for _ in range(n):
                try:
                    next(g_)
                except StopIteration:
                    pend_out["g"] = None
                    return

        def flush_out():
            step_out(10 ** 9)

        def scan_blocks(src_ps, bkey, nb, reverse, dst):
            a, b_ = scn[0], scn[1]
            av = a[:, 0:nb * 8]; bv = b_[:, 0:nb * 8]
            if nb == 1:
                MS("pool", dst, 0.0, ["scan_dst"])
                return
            if not reverse:
                MS("pool", av[:, 0:8], 0.0, ["scnA"])
                ACT(av[:, 8:nb * 8], src_ps[:, 0:(nb - 1) * 8], AF.Copy, [], [bkey, "scnA"])
            else:
                MS("pool", av[:, (nb - 1) * 8:nb * 8], 0.0, ["scnA"])
                ACT(av[:, 0:(nb - 1) * 8], src_ps[:, 8:nb * 8], AF.Copy, [], [bkey, "scnA"])
            cur, curk, oth, othk = av, "scnA", bv, "scnB"
            s_ = 1
            while s_ < nb:
                if not reverse:
                    TT("pool", oth[:, s_ * 8:nb * 8], cur[:, s_ * 8:nb * 8], cur[:, 0:(nb - s_) * 8], ALU.add,
                       [curk], [othk])
                    CP("pool", oth[:, 0:s_ * 8], cur[:, 0:s_ * 8], [curk], [othk])
                else:
                    TT("pool", oth[:, 0:(nb - s_) * 8], cur[:, 0:(nb - s_) * 8], cur[:, s_ * 8:nb * 8], ALU.add,
                       [curk], [othk])
                    CP("pool", oth[:, (nb - s_) * 8:nb * 8], cur[:, (nb - s_) * 8:nb * 8], [curk], [othk])
                cur, curk, oth, othk = oth, othk, cur, curk
                s_ *= 2
            CP("pool", dst, cur, [curk], ["scan_dst"])

        def fox_layer(l):
            j = l // 2
            rmsnorm_hT(l)
            DMA("sp", qkg_bc[:, 0:128], qg_d[j:j + 1, :].partition_broadcast(128), [], ["qkg"])
            DMA("sp", qkg_bc[:, 128:256], kg_d[j:j + 1, :].partition_broadcast(128), [], ["qkg"])
            DMA("sp", bf_bc[:], bf_d[j:j + 1, :].partition_broadcast(128), [], ["bf_bc"])
            DMA("pool", Wf[:].rearrange("p (a b) -> p a b", a=KC),
                wf_d[j].rearrange("(a p) c -> p a c", p=128), [], ["Wf"])
            Wfv = Wf[:].rearrange("p (a b) -> p a b", a=KC)
            for blk in range(NB):
                for kc in range(KC):
                    MM(b7[:, blk * 8:(blk + 1) * 8], hT[:, kc, blk * 128:(blk + 1) * 128], Wfv[:, kc, :],
                       kc == 0, kc == KC - 1, [("hT", blk), "Wf"], ["b7"])
            NF = NB * 8
            xb_, ab_, eb_, lb_ = lg[0], lg[1], lg[2], lg[3]
            TT("dve", xb_[:].rearrange("p (a b) -> p a b", b=8), b7[:, 0:NF].rearrange("p (a b) -> p a b", b=8),
               bf_bc[:].unsqueeze(1).to_broadcast([128, NB, 8]), ALU.add, ["bf_bc"], ["b7", "lg0"])
            ACT(ab_[:], xb_[:], AF.Abs, ["lg0"], ["lg1"])
            ACT(eb_[:], ab_[:], AF.Exp, ["lg1"], ["lg2"], scale=-1.0)
            ACT(lb_[:], eb_[:], AF.Ln, ["lg2"], ["lg3"], bias=1.0)
            TS("dve", ab_[:], xb_[:], 0.0, None, ALU.min, None, ["lg0"], ["lg1"])
            TT("dve", eb_[:], ab_[:], lb_[:], ALU.subtract, ["lg1", "lg3"], ["lg2"])
            logf = eb_
            DMA("sp", lp_d[j].rearrange("(n p) h -> p n h", p=128),
                logf[:, 0:NBP * 8].rearrange("p (a b) -> p a b", b=8), ["lg2"], [])
            DMA("sp", ls_d[j].rearrange("(n p) h -> p n h", p=128),
                logf[:, NBP * 8:NF].rearrange("p (a b) -> p a b", b=8), ["lg2"], [])
            MM(b7[:, 0:NBP * 8], triU[:], logf[:, 0:NBP * 8], True, True, ["triU", "lg2"], ["b7"])
            MM(b7[:, NBP * 8:NF], triS[:], logf[:, NBP * 8:NF], True, True, ["triS", "lg2"], ["b7"])
            MM(b6[:, 0:NBP * 8], ones_f[:], logf[:, 0:NBP * 8], True, True, ["ones_f", "lg2"], ["b6"])
            scan_blocks(b6, "b6", NBP, False, scd[:, 0:NBP * 8])
            TT("dve", fcum[:, 0:NBP * 8], b7[:, 0:NBP * 8], scd[:, 0:NBP * 8], ALU.add, ["scan_dst"], ["b7", "fcum"])
            ACT(fcum[:, NBP * 8:NF], b7[:, NBP * 8:NF], AF.Copy, [], ["b7", "fcum"])
            TS("dve", negf[:], fcum[:], -1.0, None, ALU.mult, None, ["fcum"], ["negf"])
            fcv = fcum[:].rearrange("p (a b) -> p a b", b=8)
            ngv = negf[:].rearrange("p (a b) -> p a b", b=8)
            for bi in range(NS):
                Lv = Lc[:].rearrange("p (a b) -> p a b", b=8)
                src = clf_d[j, bi].rearrange("(n p) h -> p n h", p=128)
                for q0 in range(0, NPB, 8):
                    q1 = min(NPB, q0 + 8)
                    DMA("sp", Lv[:, q0:q1, :], src[:, q0:q1, :], [], ["Lc"])
                MM(b7[:, 0:NPB * 8], triSL[:], Lc[:], True, True, ["triSL", "Lc"], ["b7"])
                MM(b6[:, 0:NPB * 8], ones_f[:], Lc[:], True, True, ["ones_f", "Lc"], ["b6"])
                scan_blocks(b6, "b6", NPB, True, scd[:, 0:NPB * 8])
                TT("dve", negG[bi][:].rearrange("p (h n) -> p n h", h=8),
                   b7[:, 0:NPB * 8].rearrange("p (n h) -> p n h", h=8),
                   scd[:, 0:NPB * 8].rearrange("p (n h) -> p n h", h=8), ALU.add, ["scan_dst"], ["b7", ("negG", bi)])

            for h in range(H):
                c_in = h % 2
                wi = w_acquire(("head", j, h))
                wv = wr[wi][:].rearrange("p (a b) -> p a b", a=KC)

                chunks = [(bi, ci) for bi in range(NS) for ci in range(NCH)]
                cstate = {"issued": 0}

                def issue_chunk():
                    n = cstate["issued"]
                    if n >= len(chunks):
                        return
                    cstate["issued"] = n + 1
                    bi_, ci_ = chunks[n]
                    ri_ = (h * len(chunks) + n) % 2
                    DMA("pool", KTc[ri_], ck_d[j, bi_, h, :, ci_ * CB * 128:(ci_ + 1) * CB * 128], [], [("KTc", ri_)])
                    DMA("pool", Vc[ri_], cv_d[j, bi_, h, :, ci_ * CB:(ci_ + 1) * CB, :], [], [("Vc", ri_)])

                issue_chunk()
                issue_chunk()

                RD = 5
                stg5 = [stg[0][:, :], stg[1][:, :], stg[2][:, :], tmpf[1][:, 0:384], hbf[:, :].bitcast(F32)[:, 0:384]]
                stg5k = [("stg", 0), ("stg", 1), ("stg", 2), ("tmpf", 1), "hbf"]

                def post_T(blk):
                    r5 = blk % RD
                    qb = qkb[r5]
                    TR(pT[:, 0:128], qb[:, 0:128], ident_bf[:], [("qkb", r5), "ident_bf"], ["b2"])
                    TR(pT[:, 128:256], qb[:, 128:256], ident_bf[:], [("qkb", r5), "ident_bf"], ["b2"])
                    ACT(QKT[:, :, blk * 128:(blk + 1) * 128], pT[:, 0:256].rearrange("p (a b) -> p a b", a=2),
                        AF.Copy, [], ["b2", ("QKT", blk)])

                def stageA(blk):
                    r5 = blk % RD
                    bank = (b0, b1, b7)[blk % 3]
                    bk = BK[id(bank)]
                    sg = stg5[r5]; sgk = stg5k[r5]
                    qb = qkb[r5]; qbk = ("qkb", r5)
                    s2, n2, r2 = ss2[r5], sn2[r5], rs2[r5]
                    for kc in range(KC):
                        MM(bank[:, 0:384], hT[:, kc, blk * 128:(blk + 1) * 128], wv[:, kc, 0:384],
                           kc == 0, kc == KC - 1, [("hT", blk), ("w", wi)], [bk])
                    step_out(2)
                    ACT(sg, bank[:, 0:384], AF.Copy, [], [bk, sgk])
                    ACT(qb[:, 0:128], sg[:, 0:128], AF.Square, [sgk], [qbk, ("ss2", r5)], accum=s2[:, 0:1])
                    ACT(qb[:, 128:256], sg[:, 128:256], AF.Square, [sgk], [qbk, ("ss2", r5)], accum=s2[:, 1:2])
                    TS("dve", n2[:], s2[:], 1.0 / HD, EPS, ALU.mult, ALU.add, [("ss2", r5)], [("sn2", r5)])
                    TT("pool", r2[:], n2[:], cpow[:, 0:2], ALU.pow, [("sn2", r5), "cpow"], [("rs2", r5)])

                def stageB(blk):
                    r5 = blk % RD
                    sg = stg5[r5]; sgk = stg5k[r5]
                    qb = qkb[r5]; qbk = ("qkb", r5)
                    r2 = rs2[r5]
                    STT("dve", qb[:, 0:128], sg[:, 0:128], r2[:, 0:1], qkg_bc[:, 0:128], ALU.mult, ALU.mult,
                        [sgk, ("rs2", r5), "qkg"], [qbk])
                    STT("dve", sg[:, 128:256], sg[:, 128:256], r2[:, 1:2], qkg_bc[:, 128:256], ALU.mult, ALU.mult,
                        [("rs2", r5), "qkg"], [sgk])
                    CP("pool", qb[:, 128:256], sg[:, 128:256], [sgk], [qbk])
                    CP("dve", VN[:, blk, 0:128], sg[:, 256:384], [sgk], [("VN", blk)])
                    if blk < NBP:
                        kd = kp_d[j, blk * 128:(blk + 1) * 128, h, :]
                        vd = vp_d[j, blk * 128:(blk + 1) * 128, h, :]
                    else:
                        kd = ks_d[j, (blk - NBP) * 128:(blk - NBP + 1) * 128, h, :]
                        vd = vs_d[j, (blk - NBP) * 128:(blk - NBP + 1) * 128, h, :]
                    DMA("sp", kd, sg[:, 128:256], [sgk], [])
                    DMA("sp", vd, sg[:, 256:384], [sgk], [])

                for it in range(NB + 3):
                    if it < NB:
                        stageA(it)
                    if 0 <= it - 1 < NB:
                        stageB(it - 1)
                    if 0 <= it - 3 < NB:
                        post_T(it - 3)
                flush_out()

                def gate_chunk(t0, w, gdst, gkey):
                    for kc in range(KC):
                        MM(b0[:, 0:w], wv[:, kc, 384:512], hT[:, kc, t0:t0 + w], kc == 0, kc == KC - 1,
                           [("hT", b) for b in blks_of(t0, w)] + [("w", wi)], ["b0"])
                    ACT(tmpf[0][:, 0:w], b0[:, 0:w], AF.Tanh, [], ["b0", ("tmpf", 0)], scale=0.5)
                    STT("dve", gdst[:, 0:w], tmpf[0][:, 0:w], 1.0, b0[:, 0:w], ALU.add, ALU.mult,
                        [("tmpf", 0)], ["b0", gkey])

                def finish(t0, w, c, gsrc, gkey, bo, bd, rc, rck, src_den=None, goff=0):
                    bok = BK[id(bo)]; bdk = BK[id(bd)]
                    S.add("dve", lambda e: e.reciprocal(out=rc[:, 0:w], in_=(bd[:, 0:w] if src_den is None else src_den)),
                          reads=(["dsum"] if src_den is not None else []), writes=[bdk, rck])
                    TT("pool", rc[:, 0:w], rc[:, 0:w], gsrc[:, goff:goff + w], ALU.mult, [gkey], [rck])
                    STT("dve", GT[:, c, t0:t0 + w], bo[:, 0:w], 0.5, rc[:, 0:w], ALU.mult, ALU.mult,
                        [rck], [bok] + [("GT", c, b) for b in blks_of(t0, w)])

                def fq_chunk(blk0, nblk, dst, dkey, scale=None, rows=128):
                    for i in range(nblk):
                        dg = Dg[i % 2]
                        TS("dve", dg[:], ident_f[:], fcv[:, blk0 + i, h:h + 1], None, ALU.mult, None,
                           ["ident_f", "fcum"], [("Dg", i % 2)])
                        MM(b0[:, i * 128:(i + 1) * 128], ones_f[:], dg[:], True, True, ["ones_f", ("Dg", i % 2)], ["b0"])
                    ACT(dst[0:rows, 0:nblk * 128], b0[0:rows, 0:nblk * 128], AF.Copy, [], ["b0", dkey], scale=scale)

                rel = {"n": 0, "need": (1 if NQC > 0 else 0) + 1}

                def gates_done():
                    rel["n"] += 1
                    if rel["n"] == rel["need"]:
                        w_release()

                def prompt_gen():
                    pt_i = 0
                    bo, bd = b5, b6
                    bok, bdk = "b5", "b6"
                    hp_all = Fq[1].bitcast(BF16)
                    hps = [hp_all[:, 0:512], hp_all[:, 512:1024]]

                    def fq_prompt(qc_):
                        hp = hps[qc_ % 2]; hk = ("FqHp", qc_ % 2)
                        fq_chunk(4 * qc_, 4, Fq[0], ("Fq", 0), scale=1.0 / SCALE, rows=64)
                        CP("dve", hp[0:64, :], Fq[0][0:64, :], [("Fq", 0)], [hk])
                        TT("dve", Fq[0][32:64, :], Fq[0][32:64, :], hp[32:64, :], ALU.subtract, [hk], [("Fq", 0)])
                        CP("dve", hp[32:64, :], Fq[0][32:64, :], [("Fq", 0)], [hk])

                    if NQC > 0:
                        fq_prompt(0)
                    for qc in range(NQC):
                        t0 = qc * 512
                        hp = hps[qc % 2]; hk = ("FqHp", qc % 2)
                        gate_chunk(t0, 512, gsc[qc % 2], ("gsc", qc % 2))
                        if qc == NQC - 1:
                            gates_done()
                        yield
                        if qc + 1 < NQC:
                            fq_prompt(qc + 1)
                        nkb = 4 * qc + 4
                        pend = None

                        def pv(kb, c0, pti):
                            ptile = PT[pti]
                            MM(bo[:, c0:512], VN[:, kb, 0:128], ptile[:, c0:512], kb == 0, kb == nkb - 1,
                               [("VN", kb), ("PT", pti)], [bok])
                            MM(bd[:, c0:512], ones_bf[:], ptile[:, c0:512], kb == 0, kb == nkb - 1,
                               ["ones_bf", ("PT", pti)], [bdk])

                        for kb in range(nkb):
                            c0 = max(0, kb - 4 * qc) * 128
                            st = (b3, b4)[kb % 2]; stk = BK[id(st)]
                            diag = kb >= 4 * qc
                            MM(st[:, c0:512], QKT[:, 1, kb * 128:(kb + 1) * 128], QKT[:, 0, t0 + c0:t0 + 512], True, False,
                               [("QKT", kb)] + [("QKT", b) for b in blks_of(t0 + c0, 512 - c0)], [stk])
                            MM(st[:, c0:512], selbf[0:33, :], hp[0:33, c0:512], False, not diag, ["selbf", hk], [stk])
                            if diag:
                                MM(st[:, c0:c0 + 128], ident_bf[:], caus_bf[:], False, True, ["ident_bf", "caus_bf"], [stk])
                            if pend is not None:
                                pv(*pend)
                            pti = pt_i % 2
                            pt_i += 1
                            ACT(PT[pti][:, c0:512], st[:, c0:512], AF.Exp, ["negf"], [stk, ("PT", pti)],
                                bias=ngv[:, kb, h:h + 1], scale=SCALE)
                            pend = (kb, c0, pti)
                            yield
                        pv(*pend)
                        finish(t0, 512, c_in, gsc[qc % 2], ("gsc", qc % 2), bo, bd, rcg, "rcg")
                        yield

                def sample_gen():
                    cn = 0
                    bo, bd = b1, b7
                    bok, bdk = "b1", "b7"
                    st = b2; stk = "b2"
                    for jj_ in range(NBS):
                        gate_chunk(TP + 128 * jj_, 128, gscS[:, 128 * jj_:128 * (jj_ + 1)], "gscS")
                    gates_done()
                    yield
                    for bi in range(NS):
                        jj, r = bi // 2, bi % 2
                        sblk = NBP + jj
                        q0 = TP + 128 * jj + 64 * r
                        if r == 0:
                            fq_chunk(sblk, 1, FqS, "FqS", scale=1.0 / SCALE)
                            CP("dve", FqH[0:64, :], FqS[0:64, :], ["FqS"], ["FqH"])
                            TT("dve", FqL[32:64, :], FqS[32:64, :], FqH[32:64, :], ALU.subtract, ["FqS", "FqH"], ["FqL"])
                            CP("dve", FqH[32:64, :], FqL[32:64, :], ["FqL"], ["FqH"])
                            yield
                        ngb = negG[bi][:].rearrange("p (h n) -> p h n", h=8)
                        first = True
                        for ci in range(NCH):
                            ri = (h * len(chunks) + cn) % 2
                            cn += 1
                            for a in range(CB):
                                MM(st[:, a * 64:(a + 1) * 64], KTc[ri][:, a * 128:(a + 1) * 128], QKT[:, 0, q0:q0 + 64],
                                   a == 0, False, [("KTc", ri), ("QKT", sblk)], [stk])
                            MM(st[:, 0:CB * 64].rearrange("p (a b) -> p a b", a=CB), selbf[0:33, :],
                               FqH[0:33, 64 * r:64 * r + 64].unsqueeze(1).to_broadcast([33, CB, 64]), False, True,
                               ["selbf", "FqH"], [stk])
                            STT("dve", st[:, 0:CB * 64].rearrange("p (a b) -> p a b", a=CB),
                                st[:, 0:CB * 64].rearrange("p (a b) -> p a b", a=CB), SCALE,
                                ngb[:, h, ci * CB:(ci + 1) * CB].unsqueeze(2).to_broadcast([128, CB, 64]),
                                ALU.mult, ALU.add, [("negG", bi)], [stk])
                            ACT(PTs[:, 0:CB * 64], st[:, 0:CB * 64], AF.Exp, [], [stk, "PTs"])
                            yield
                            for a in range(CB):
                                MM(bo[:, 0:64], Vc[ri][:, a, :], PTs[:, a * 64:(a + 1) * 64], first, False,
                                   [("Vc", ri), "PTs"], [bok])
                                first = False
                            MM(bd[:, 0:CB * 64], ones_bf[:], PTs[:, 0:CB * 64], ci == 0, False,
                               ["ones_bf", "PTs"], [bdk])
                            issue_chunk()
                            yield
                        MM(st[:, 0:64], QKT[:, 1, sblk * 128:(sblk + 1) * 128], QKT[:, 0, q0:q0 + 64], True, False,
                           [("QKT", sblk)], [stk])
                        MM(st[:, 0:64], selbf[0:33, :], FqH[0:33, 64 * r:64 * r + 64], False, True, ["selbf", "FqH"], [stk])
                        STT("dve", st[:, 0:64], st[:, 0:64], SCALE, smask[r][:], ALU.mult, ALU.add, [("smask", r)], [stk])
                        ACT(PTs[:, 0:64], st[:, 0:64], AF.Exp, ["negf"], [stk, "PTs"], bias=ngv[:, sblk, h:h + 1])
                        yield
                        MM(bo[:, 0:64], VN[:, sblk, 0:128], PTs[:, 0:64], first, True, [("VN", sblk), "PTs"], [bok])
                        MM(bd[:, 0:64], ones_bf[:], PTs[:, 0:64], NCH == 0, True, ["ones_bf", "PTs"], [bdk])
                        if NCH > 0:
                            S.add("dve", (lambda bd_: (lambda e: e.tensor_reduce(
                                out=FqL[:, 0:64], in_=bd_[:, 0:CB * 64].rearrange("p (a q) -> p q a", a=CB),
                                axis=mybir.AxisListType.X, op=ALU.add)))(bd), reads=[], writes=[bdk, "dsum", "FqL"])
                            finish(q0, 64, c_in, gscS, "gscS", bo, bd, rcgS, "rcgS", src_den=FqL[:, 0:64], goff=128 * jj + 64 * r)
                        else:
                            finish(q0, 64, c_in, gscS, "gscS", bo, bd, rcgS, "rcgS", goff=128 * jj + 64 * r)
                        yield

                gens = [prompt_gen(), sample_gen()]
                quota = [1, 1]
                alive = [True, True]
                while any(alive):
                    for gi_ in range(2):
                        if not alive[gi_]:
                            continue
                        for _ in range(quota[gi_]):
                            try:
                                next(gens[gi_])
                            except StopIteration:
                                alive[gi_] = False
                                break

                if h % 2 == 1:
                    flush_out()
                    pend_out["g"] = out_round_gen(("outa", j, h // 2))
            flush_out()

        def gmlp_layer(l):
            j = l // 2
            rmsnorm_hT(l)
            DMA("sp", wsf, ws_d[j].rearrange("g t s -> t g s"), [], ["wsf"])
            MS("pool", wsf[0:64, :, 64:128], 0.0, ["wsf"])
            CP("pool", wsb, wsf, ["wsf"], ["wsb"])
            for g in range(8):
                TR(pT[:, g * 128:(g + 1) * 128], wsb[:, g, :], ident_bf[:], ["wsb", "ident_bf"], ["b2"])
            ACT(WsTp, pT[:, :].rearrange("p (a b) -> p a b", a=8), AF.Copy, [], ["b2", "WsTp"])
            MS("pool", wsf, 0.0, ["wsf"])
            DMA("sp", wsf[0:64, :, 0:64], ws_d[j, :, 0:64, 0:64].rearrange("g t s -> t g s"), [], ["wsf"])
            DMA("sp", wsf[64:128, :, 64:128], ws_d[j, :, 0:64, 0:64].rearrange("g t s -> t g s"), [], ["wsf"])
            CP("pool", wsb, wsf, ["wsf"], ["wsb"])
            for g in range(8):
                TR(pT[:, g * 128:(g + 1) * 128], wsb[:, g, :], ident_bf[:], ["wsb", "ident_bf"], ["b2"])
            ACT(WsTs, pT[:, :].rearrange("p (a b) -> p a b", a=8), AF.Copy, [], ["b2", "WsTs"])
            DMA("sp", bsf[0:2, :], bs_d[j:j + 1, :].partition_broadcast(2), [], ["bsf"])
            CP("pool", bsh[0:2, :], bsf[0:2, :], ["bsf"], ["bsh"])
            TT("pool", bsl[0:2, :], bsf[0:2, :], bsh[0:2, :], ALU.subtract, ["bsf", "bsh"], ["bsl"])
            CP("pool", bsr[0:2, :], bsl[0:2, :], ["bsl"], ["bsr"])
            CP("pool", bsr[0:1, :], bsh[0:1, :], ["bsh"], ["bsr"])
            bsrv = bsr[:].rearrange("p (g t) -> p g t", g=8)
            bsrsv = bsrs[:].rearrange("p (g t) -> p g t", g=8)
            CP("pool", bsrsv[0:2, :, 0:64], bsrv[0:2, :, 0:64], ["bsr"], ["bsrs"])
            CP("pool", bsrsv[0:2, :, 64:128], bsrv[0:2, :, 0:64], ["bsr"], ["bsrs"])
            DMA("sp", vg_bc[:], vg_d[j:j + 1, :].partition_broadcast(128), [], ["vg_bc"])

            for g in range(8):
                wb_ = w_acquire(("gB", j, g))
                wbv = wr[wb_][:].rearrange("p (a b) -> p a b", a=KC)
                def vA(blk):
                    r3 = blk % 3
                    bank = (b0, b1, b7)[r3]; bk = BK[id(bank)]
                    qb = qkb[r3]; qbk = ("qkb", r3)
                    s2, n2, r2 = ss2[r3], sn2[r3], rs2[r3]
                    for kc in range(KC):
                        MM(bank[:, 0:256], hT[:, kc, blk * 128:(blk + 1) * 128], wbv[:, kc, 0:256],
                           kc == 0, kc == KC - 1, [("hT", blk), ("w", wb_)], [bk])
                    step_out(2)
                    ACT(qb[:, 0:256], bank[:, 0:256], AF.Square, [], [bk, qbk, ("ss2", r3)], accum=s2[:, 0:1])
                    TS("dve", n2[:, 0:1], s2[:, 0:1], 1.0 / 256, EPS, ALU.mult, ALU.add, [("ss2", r3)], [("sn2", r3)])
                    TT("pool", r2[:, 0:1], n2[:, 0:1], cpow[:, 0:1], ALU.pow, [("sn2", r3), "cpow"], [("rs2", r3)])

                def vB(blk):
                    r3 = blk % 3
                    bank = (b0, b1, b7)[r3]; bk = BK[id(bank)]
                    sg = stg[blk % 2]; sgk = ("stg", blk % 2)
                    r2 = rs2[r3]
                    if blk < NBP:
                        STT("dve", VN[:, blk, :], bank[:, 0:256], r2[:, 0:1], vg_bc[:, g * 256:(g + 1) * 256],
                            ALU.mult, ALU.mult, [("rs2", r3), "vg_bc"], [bk, ("VN", blk)])
                    else:
                        STT("dve", sg[:, 0:256], bank[:, 0:256], r2[:, 0:1], vg_bc[:, g * 256:(g + 1) * 256],
                            ALU.mult, ALU.mult, [("rs2", r3), "vg_bc"], [bk, sgk])
                        CP("pool", VN[:, blk, :], sg[:, 0:256], [sgk], [("VN", blk)])
                        DMA("sp", sgu_d[j, (blk - NBP) * 128:(blk - NBP + 1) * 128, g * 256:(g + 1) * 256],
                            sg[:, 0:256], [sgk], [])

                for it in range(NB + 1):
                    if it < NB:
                        vA(it)
                    if it >= 1:
                        vB(it - 1)
                w_release()
                wa = w_acquire(("gA", j, g))
                wav = wr[wa][:].rearrange("p (a b) -> p a b", a=KC)
                ugi = 0
                for fc in range(2):
                    for ti, (t0, w) in enumerate(tchunks):
                        hk = [("hT", b) for b in blks_of(t0, w)]
                        bu, bg = ((b3, b4), (b7, b2))[ugi % 2]
                        buk, bgk = BK[id(bu)], BK[id(bg)]
                        tf = tmpf[ugi % 2]; tfk = ("tmpf", ugi % 2)
                        ugi += 1
                        for kc in range(KC):
                            MM(bg[:, 0:w], wav[:, kc, 256 + fc * 128:256 + (fc + 1) * 128], hT[:, kc, t0:t0 + w],
                               kc == 0, kc == KC - 1, hk + [("w", wa)], [bgk])
                        ACT(tf[:, 0:w], bg[:, 0:w], AF.Tanh, [], [bgk, tfk], scale=0.5)
                        for kc in range(KC):
                            MM(bu[:, 0:w], wav[:, kc, fc * 128:(fc + 1) * 128], hT[:, kc, t0:t0 + w],
                               kc == 0, kc == KC - 1, hk + [("w", wa)], [buk])
                        STT("dve", tf[:, 0:w], tf[:, 0:w], 1.0, bg[:, 0:w], ALU.add, ALU.mult, [], [bgk, tfk])
                        TT("dve", QKT[:, fc, t0:t0 + w], bu[:, 0:w], tf[:, 0:w], ALU.mult, [tfk],
                           [buk] + [("QKT", b) for b in blks_of(t0, w)])
                w_release()
                gi = 0
                for fc in range(2):
                    for (t0, w) in tchunks:
                        bank = (b5, b6)[gi % 2]; bk = BK[id(bank)]
                        gi += 1
                        for i, blk in enumerate(blks_of(t0, w)):
                            wst = WsTp if blk < NBP else WsTs
                            wsk = "WsTp" if blk < NBP else "WsTs"
                            brow = bsrv if blk < NBP else bsrsv
                            brk = "bsr" if blk < NBP else "bsrs"
                            MM(bank[:, i * 128:(i + 1) * 128], VN[:, blk, fc * 128:(fc + 1) * 128], wst[:, g, :],
                               True, False, [("VN", blk), wsk], [bk])
                            MM(bank[:, i * 128:(i + 1) * 128], ones_bf[0:2, :], brow[0:2, g, :], False, True,
                               ["ones_bf", brk], [bk])
                        STT("dve", GT[:, fc, t0:t0 + w], bank[:, 0:w], 0.5, QKT[:, fc, t0:t0 + w], ALU.mult, ALU.mult,
                            [("QKT", b) for b in blks_of(t0, w)], [bk] + [("GT", fc, b) for b in blks_of(t0, w)])
                flush_out()
                pend_out["g"] = out_round_gen(("outb", j, g))
            flush_out()

        def fence(to_fox):
            old = GM_KEYS if to_fox else FOX_KEYS
            new = FOX_KEYS if to_fox else GM_KEYS
            S.add("pool", lambda e: e.memset(ss[:, 31:32], 0.0), reads=[], writes=list(old) + list(new) + ["fence"])

        FOX_KEYS = ([("Vc", i) for i in range(2)] + [("KTc", i) for i in range(2)] +
                    [("PT", i) for i in range(2)] + [("Fq", i) for i in range(2)] + [("FqHp", i) for i in range(2)] + ["FqS", "FqH", "FqL", "rcg", "rcgS", "PTs", "gscS", "dsum"] +
                    [("Dg", i) for i in range(2)] + [("gsc", i) for i in range(2)] + [("negG", i) for i in range(NS)] +
                    ["Lc", "scnA", "scnB", "qkg", "bf_bc", "Wf", "lg0", "lg1", "lg2", "lg3", "fcum", "negf", "scan_dst"])
        GM_KEYS = ["wsf", "wsb", "WsTp", "WsTs", "vg_bc", "bsf", "bsh", "bsl", "bsr", "bsrs"]

        for l in range(depth):
            if l % 2 == 0:
                if l > 0:
                    fence(True)
                fox_layer(l)
            else:
                fence(False)
                gmlp_layer(l)

        ypv = yp_d.rearrange("(n p) d -> p n d", p=128)
        for n0 in range(0, NBP, 4):
            DMA("sp", ypv[:, n0:n0 + 4, :], x[:, n0:n0 + 4, :], [xk(b) for b in range(n0, n0 + 4)], [])
        DMA("sp", ys_d.rearrange("(n p) d -> p n d", p=128), x[:, NBP:NB, :], [xk(b) for b in range(NBP, NB)], [])

        S.emit(nc, es)
    return nc


_CACHE = {}


def kernel(x_prompt, x_sample, cache_k, cache_v, cache_logf, norm_g, w_in_a, b_f, q_g, k_g, w_out_a,
           w_in_b, v_g, ws, bs, w_out_b):
    f = lambda a: np.ascontiguousarray(np.asarray(a, dtype=np.float32))
    B, TP, _ = x_prompt.shape
    DB, DS, _ = x_sample.shape
    PAST = cache_k.shape[2]
    depth = norm_g.shape[0]
    n = N_CORES
    NS = DB // n
    key = (TP, NS, PAST, depth)
    if key not in _CACHE:
        _CACHE[key] = build(TP=TP, NS=NS, PAST=PAST, depth=depth)
    nc = _CACHE[key]
    NA = (depth + 1) // 2
    NBL = depth // 2
    wia = np.asarray(w_in_a, dtype=np.float32)
    wia_h = f(wia[:, :, :4096].reshape(NA, D, 4, H, HD).transpose(0, 3, 1, 2, 4).reshape(NA, H, D, 512))
    wib = np.asarray(w_in_b, dtype=np.float32)
    wib4 = wib.reshape(NBL, D, 3, 8, 256)
    wib_g = f(np.stack([wib4[:, :, 0], wib4[:, :, 2], wib4[:, :, 1]], axis=2).transpose(0, 3, 1, 2, 4).reshape(NBL, 8, D, 768))
    shared = {"norm_g": f(norm_g), "w_in_a_h": wia_h, "w_f": f(wia[:, :, 4096:4104]), "b_f": f(b_f), "q_g": f(q_g), "k_g": f(k_g),
              "w_out_a": f(w_out_a), "w_in_b_g": wib_g, "v_g": f(np.asarray(v_g).reshape(NBL, -1)),
              "ws": f(ws), "bs": f(np.asarray(bs).reshape(NBL, -1)), "w_out_b": f(w_out_b)}
    in_maps = []
    for c in range(n):
        m = dict(shared)
        m["xp"] = f(x_prompt[c])
        m["xs"] = f(np.asarray(x_sample[c * NS:(c + 1) * NS]).reshape(NS * DS, -1))
        m["ckT"] = f(np.asarray(cache_k[:, c * NS:(c + 1) * NS]).transpose(0, 1, 3, 4, 2))
        cvc = np.asarray(cache_v[:, c * NS:(c + 1) * NS])
        m["cvh"] = f(cvc.reshape(NA, NS, PAST // 128, 128, H, HD).transpose(0, 1, 4, 3, 2, 5))
        m["clf"] = f(cache_logf[:, c * NS:(c + 1) * NS])
        in_maps.append(m)
    res = run_bass_kernel_spmd(nc, in_maps, core_ids=list(range(n)))
    R = res.results
    y_prompt = np.stack([R[c]["yp"] for c in range(n)], 0)
    y_sample = np.concatenate([R[c]["ys"].reshape(NS, DS, D) for c in range(n)], 0)
    k_prompt = np.stack([R[c]["kp"] for c in range(n)], 1)
    v_prompt = np.stack([R[c]["vp"] for c in range(n)], 1)
    logf_prompt = np.stack([R[c]["lp"] for c in range(n)], 1)
    k_sample = np.concatenate([R[c]["ks"].reshape(NA, NS, DS, H, HD) for c in range(n)], 1)
    v_sample = np.concatenate([R[c]["vs"].reshape(NA, NS, DS, H, HD) for c in range(n)], 1)
    logf_sample = np.concatenate([R[c]["ls"].reshape(NA, NS, DS, H) for c in range(n)], 1)
    sgu = np.concatenate([R[c]["sgu"].reshape(NBL, NS, DS, 2048) for c in range(n)], 1)
    return (y_prompt, y_sample, k_prompt, v_prompt, logf_prompt, k_sample, v_sample, logf_sample, sgu)
```
